# Optimizing a Trainium2 kernel written in Bass

```python
import math
import jax, jax.numpy as jnp
from jax import lax
import numpy as np

D_MODEL = 1024
BATCH = 2
SEQ = 16384
DEPTH = 2

HEAD_DIM = 64
GRID_W = 64
NA_HEADS = 4
SW_HEADS = 6
SW_KV_HEADS = 2
AX_HEADS = 6
AX_KV_HEADS = 2
MIX_WIDTH = (NA_HEADS + SW_HEADS + AX_HEADS) * HEAD_DIM
NA_WIN_ROWS = 8
NA_WIN_COLS = 16
SW_RADIUS = 128
BLOCK = 128
T5_BUCKETS = 32
T5_MAX_DIST = 128
ROPE_THETA = 10000.0
FFN_HIDDEN = ((8 * D_MODEL + 3 * 256 - 1) // (3 * 256)) * 256
IN_SPLITS = (NA_HEADS * HEAD_DIM, NA_HEADS * HEAD_DIM, NA_HEADS * HEAD_DIM,
             SW_HEADS * HEAD_DIM, SW_KV_HEADS * HEAD_DIM, SW_KV_HEADS * HEAD_DIM,
             AX_HEADS * HEAD_DIM, AX_KV_HEADS * HEAD_DIM, AX_KV_HEADS * HEAD_DIM)
IN_WIDTH = sum(IN_SPLITS)
GROUP_WIDTHS = (NA_HEADS * HEAD_DIM, SW_HEADS * HEAD_DIM, AX_HEADS * HEAD_DIM)
EPS = 1e-6
NEG_INF = -1e30

kernel_name = "hybrid_parallel_heads_encoder"


def rms_norm(x, g):
    xf = x.astype(jnp.float32)
    y = xf * lax.rsqrt(jnp.mean(xf * xf, axis=-1, keepdims=True) + EPS)
    return (y * g.astype(jnp.float32)).astype(x.dtype)


def split_cols(t, sizes):
    out = []
    start = 0
    for s in sizes:
        out.append(t[..., start:start + s])
        start += s
    return out


def neighborhood_attention(q, k, v, rpb):
    B, S, H, d = q.shape
    rows = S // GRID_W
    kh = min(NA_WIN_ROWS, rows)
    kw = NA_WIN_COLS
    qg = q.reshape(B, rows, GRID_W, H, d)
    kg = k.reshape(B, rows, GRID_W, H, d)
    vg = v.reshape(B, rows, GRID_W, H, d)
    cols = jnp.arange(GRID_W)
    col_start = jnp.clip(cols - kw // 2, 0, GRID_W - kw)
    col_idx = col_start[:, None] + jnp.arange(kw)[None, :]
    col_off = col_idx - cols[:, None] + (NA_WIN_COLS - 1)
    scale = d ** -0.5

    def one_row(r):
        rs = jnp.clip(r - kh // 2, 0, rows - kh)
        k_band = lax.dynamic_slice_in_dim(kg, rs, kh, axis=1)
        v_band = lax.dynamic_slice_in_dim(vg, rs, kh, axis=1)
        k_nb = k_band[:, :, col_idx]
        v_nb = v_band[:, :, col_idx]
        q_row = lax.dynamic_index_in_dim(qg, r, axis=1, keepdims=False)
        s = jnp.einsum('bqhd,brqwhd->bhqrw', q_row, k_nb).astype(jnp.float32) * scale
        row_off = rs + jnp.arange(kh) - r + (NA_WIN_ROWS - 1)
        bias = rpb[:, row_off[None, :, None], col_off[:, None, :]]
        s = s + bias[None].astype(jnp.float32)
        p = jax.nn.softmax(s.reshape(B, H, GRID_W, kh * kw), axis=-1)
        p = p.reshape(B, H, GRID_W, kh, kw).astype(v.dtype)
        return jnp.einsum('bhqrw,brqwhd->bqhd', p, v_nb)

    out = lax.map(one_row, jnp.arange(rows))
    return out.transpose(1, 0, 2, 3, 4).reshape(B, S, H * d)


def t5_bucket(rel):
    nb = T5_BUCKETS // 2
    ret = (rel > 0).astype(jnp.int32) * nb
    n = jnp.abs(rel)
    max_exact = nb // 2
    nf = jnp.maximum(n, max_exact).astype(jnp.float32)
    large = max_exact + (jnp.log(nf / max_exact) / math.log(T5_MAX_DIST / max_exact)
                         * (nb - max_exact)).astype(jnp.int32)
    large = jnp.minimum(large, nb - 1)
    return ret + jnp.where(n < max_exact, n, large)


def sliding_window_attention(q, k, v, sink, t5_table):
    B, S, H, d = q.shape
    G = k.shape[2]
    R = H // G
    nb = S // BLOCK
    scale = d ** -0.5
    qb = q.reshape(B, nb, BLOCK, G, R, d)

    def band(t):
        tb = t.reshape(B, nb, BLOCK, G, d)
        pad = jnp.zeros_like(tb[:, :1])
        tp = jnp.concatenate([pad, tb, pad], axis=1)
        return jnp.concatenate([tp[:, :-2], tp[:, 1:-1], tp[:, 2:]], axis=2)

    kb = band(k)
    vb = band(v)
    qpos = jnp.arange(BLOCK)
    kpos = jnp.arange(3 * BLOCK) - BLOCK
    rel = kpos[None, :] - qpos[:, None]
    bias = t5_table[t5_bucket(rel)].astype(jnp.float32)
    bias = jnp.transpose(bias, (2, 0, 1)).reshape(G, R, BLOCK, 3 * BLOCK)
    kabs = jnp.arange(nb)[:, None] * BLOCK + kpos[None, :]
    valid = (jnp.abs(rel) <= SW_RADIUS)[None] & ((kabs >= 0) & (kabs < S))[:, None, :]
    s = jnp.einsum('bnqgrd,bnkgd->bngrqk', qb, kb).astype(jnp.float32) * scale + bias
    s = jnp.where(valid[None, :, None, None], s, NEG_INF)
    sink_logits = jnp.broadcast_to(sink.reshape(G, R, 1, 1).astype(jnp.float32), s.shape[:-1] + (1,))
    p = jax.nn.softmax(jnp.concatenate([s, sink_logits], axis=-1), axis=-1)[..., :-1]
    o = jnp.einsum('bngrqk,bnkgd->bnqgrd', p.astype(v.dtype), vb)
    return o.reshape(B, S, H * d)


def axial_rope_tables(S):
    t = jnp.arange(S)
    row = (t // GRID_W).astype(jnp.float32)
    col = (t % GRID_W).astype(jnp.float32)
    axis_dim = HEAD_DIM // 2
    freqs = ROPE_THETA ** (-jnp.arange(0, axis_dim, 2, dtype=jnp.float32) / axis_dim)
    ang = jnp.stack([row[:, None] * freqs, col[:, None] * freqs], axis=1)
    return jnp.cos(ang), jnp.sin(ang)


def apply_axial_rope(x, cos, sin):
    B, S, H, d = x.shape
    xr = x.reshape(B, S, H, 2, 2, d // 4).astype(jnp.float32)
    x1 = xr[..., 0, :]
    x2 = xr[..., 1, :]
    c = cos[None, :, None]
    s = sin[None, :, None]
    out = jnp.stack([x1 * c - x2 * s, x2 * c + x1 * s], axis=-2)
    return out.reshape(B, S, H, d).astype(x.dtype)


def axial_attention(q, k, v, gq, gk):
    B, S, H, d = q.shape
    G = k.shape[2]
    R = H // G
    nb = S // BLOCK
    scale = d ** -0.5
    cos, sin = axial_rope_tables(S)
    q = apply_axial_rope(rms_norm(q, gq), cos, sin)
    k = apply_axial_rope(rms_norm(k, gk), cos, sin)
    qb = q.reshape(B, nb, BLOCK, G, R, d).transpose(1, 0, 2, 3, 4, 5)

    def one_block(qblk):
        s = jnp.einsum('bqgrd,bkgd->bgrqk', qblk, k).astype(jnp.float32) * scale
        p = jax.nn.softmax(s, axis=-1).astype(v.dtype)
        return jnp.einsum('bgrqk,bkgd->bqgrd', p, v)

    o = lax.map(one_block, qb)
    return o.transpose(1, 0, 2, 3, 4, 5).reshape(B, S, H * d)


def setup_inputs(seed: int = 0) -> dict:
    key = jax.random.key(seed)
    ks = jax.random.split(key, 18)
    D = D_MODEL
    L = DEPTH
    F = FFN_HIDDEN

    def nrm(k, shape, scale):
        return jax.random.normal(k, shape, jnp.float32) * scale

    return {
        "x": nrm(ks[0], (BATCH, SEQ, D), 1.0),
        "c": nrm(ks[1], (BATCH, D), 1.0),
        "w_mod": nrm(ks[2], (L, D, 6 * D), 0.5 * D ** -0.5),
        "b_mod": nrm(ks[3], (L, 6 * D), 0.01),
        "g_attn": 1.0 + nrm(ks[4], (L, D), 0.01),
        "w_in": nrm(ks[5], (L, D, IN_WIDTH), D ** -0.5),
        "rpb_na": nrm(ks[6], (L, NA_HEADS, 2 * NA_WIN_ROWS - 1, 2 * NA_WIN_COLS - 1), 0.1),
        "sink_sw": nrm(ks[7], (L, SW_HEADS), 0.5),
        "t5_table": nrm(ks[8], (T5_BUCKETS, SW_HEADS), 0.1),
        "gq_ax": 1.0 + nrm(ks[9], (L, HEAD_DIM), 0.01),
        "gk_ax": 1.0 + nrm(ks[10], (L, HEAD_DIM), 0.01),
        "g_group": 1.0 + nrm(ks[11], (L, MIX_WIDTH), 0.01),
        "w_o": nrm(ks[12], (L, MIX_WIDTH, D), MIX_WIDTH ** -0.5),
        "g_ffn": 1.0 + nrm(ks[13], (L, D), 0.01),
        "w_gu": nrm(ks[14], (L, D, 2 * F), D ** -0.5),
        "w_down": nrm(ks[15], (L, F, D), F ** -0.5),
        "g_final": 1.0 + nrm(ks[16], (D,), 0.01),
    }


def reference(x, c, w_mod, b_mod, g_attn, w_in, rpb_na, sink_sw, t5_table, gq_ax, gk_ax,
              g_group, w_o, g_ffn, w_gu, w_down, g_final):
    B, S, D = x.shape
    cond = jax.nn.silu(c)
    for l in range(DEPTH):
        mod = cond @ w_mod[l] + b_mod[l]
        sh_a, sc_a, gt_a, sh_f, sc_f, gt_f = [m[:, None, :] for m in split_cols(mod, (D,) * 6)]

        h = rms_norm(x, g_attn[l]) * (1 + sc_a) + sh_a
        proj = h @ w_in[l]
        qa, ka, va, qb, kb, vb, qc, kc, vc = split_cols(proj, IN_SPLITS)
        ya = neighborhood_attention(qa.reshape(B, S, NA_HEADS, HEAD_DIM),
                                    ka.reshape(B, S, NA_HEADS, HEAD_DIM),
                                    va.reshape(B, S, NA_HEADS, HEAD_DIM), rpb_na[l])
        yb = sliding_window_attention(qb.reshape(B, S, SW_HEADS, HEAD_DIM),
                                      kb.reshape(B, S, SW_KV_HEADS, HEAD_DIM),
                                      vb.reshape(B, S, SW_KV_HEADS, HEAD_DIM),
                                      sink_sw[l], t5_table)
        yc = axial_attention(qc.reshape(B, S, AX_HEADS, HEAD_DIM),
                             kc.reshape(B, S, AX_KV_HEADS, HEAD_DIM),
                             vc.reshape(B, S, AX_KV_HEADS, HEAD_DIM), gq_ax[l], gk_ax[l])
        ga, gb, gc = split_cols(g_group[l], GROUP_WIDTHS)
        y = jnp.concatenate([rms_norm(ya, ga), rms_norm(yb, gb), rms_norm(yc, gc)], axis=-1)
        x = x + gt_a * (y @ w_o[l])

        h = rms_norm(x, g_ffn[l]) * (1 + sc_f) + sh_f
        gate, up = split_cols(h @ w_gu[l], (FFN_HIDDEN, FFN_HIDDEN))
        x = x + gt_f * ((jax.nn.silu(gate) * up) @ w_down[l])
    return rms_norm(x, g_final)
```

```python
import math
import numpy as np
import ml_dtypes
from contextlib import ExitStack
import concourse.bass as bass
import concourse.mybir as mybir
from concourse.bass_utils import run_bass_kernel_spmd

F32 = mybir.dt.float32
BF16 = mybir.dt.bfloat16
I32 = mybir.dt.int32
AF = mybir.ActivationFunctionType
ALU = mybir.AluOpType
AXL = mybir.AxisListType

D = 1024
T = 4096
NB = 32
NG = 8
F = 2816
NFC = 22
L = 2
EPS = 1e-6
MASKV = -30000.0


class Res:
    __slots__ = ("name", "w", "r", "dsem")

    def __init__(self, name, dsem=None):
        self.name = name
        self.w = None
        self.r = {}
        self.dsem = dsem


class Sched:
    def __init__(self, nc, stack):
        self.nc = nc
        self.stack = stack
        self.eng = {"pe": nc.tensor, "act": nc.scalar, "dve": nc.vector,
                    "pool": nc.gpsimd, "sp": nc.sync}
        self.sems = {}
        self.cnt = {}
        self.seen = {e: {} for e in self.eng}
        for e in self.eng:
            self.sems[e] = stack.enter_context(nc.semaphore("s_" + e))
            self.cnt[e] = 0
        self.nd = 0

    def res(self, name, dma=False):
        r = Res(name)
        if dma:
            key = "d%d_%s" % (self.nd, name)
            self.nd += 1
            self.sems[key] = self.stack.enter_context(self.nc.semaphore(key))
            self.cnt[key] = 0
            r.dsem = key
        return r

    def _wait(self, e, deps):
        E = self.eng[e]
        seen = self.seen[e]
        for k, v in deps.items():
            if e == "pe" and k == "pe":
                continue
            if seen.get(k, 0) < v:
                if k not in self.eng:
                    v = self.cnt[k]
                E.wait_ge(self.sems[k], v)
                seen[k] = v

    def op(self, e, fn, reads=(), writes=()):
        deps = {}
        for r in reads:
            if r.w is not None:
                k, v = r.w
                if deps.get(k, 0) < v:
                    deps[k] = v
        for w in writes:
            if w.w is not None:
                k, v = w.w
                if deps.get(k, 0) < v:
                    deps[k] = v
            for k, v in w.r.items():
                if deps.get(k, 0) < v:
                    deps[k] = v
        self._wait(e, deps)
        ins = fn(self.eng[e])
        ins.then_inc(self.sems[e], 1)
        self.cnt[e] += 1
        n = self.cnt[e]
        for r in reads:
            if r.r.get(e, 0) < n:
                r.r[e] = n
        for w in writes:
            w.w = (e, n)
            w.r = {}
        return ins

    def dma(self, e, out, in_, reads, writes, indirect=None, **kw):
        dst = writes[0]
        assert dst.dsem is not None, dst.name
        deps = {}
        for r in reads:
            if r.w is not None:
                k, v = r.w
                if deps.get(k, 0) < v:
                    deps[k] = v
        for w in writes:
            if w.w is not None and w.w[0] != dst.dsem:
                k, v = w.w
                if deps.get(k, 0) < v:
                    deps[k] = v
            for k, v in w.r.items():
                if deps.get(k, 0) < v:
                    deps[k] = v
        self._wait(e, deps)
        if indirect is not None:
            ins = self.eng[e].indirect_dma_start(out=out, out_offset=None, in_=in_, in_offset=indirect)
        else:
            ins = self.eng[e].dma_start(out=out, in_=in_, **kw)
        ins.then_inc(self.sems[dst.dsem], 16)
        self.cnt[dst.dsem] += 16
        n = self.cnt[dst.dsem]
        for r in reads:
            if r.r.get(dst.dsem, 0) < n:
                r.r[dst.dsem] = n
        for w in writes:
            w.w = (dst.dsem, n)
            w.r = {}
        return ins

    def barrier(self):
        tot = {k: v for k, v in self.cnt.items() if v > 0}
        for e in self.eng:
            self._wait(e, dict(tot))


QA, KA, VA, QB, KB, VB, QC, KC, VC = 0, 256, 512, 768, 1152, 1280, 1408, 1792, 1920


def _win_perm():
    p = []
    p += list(range(QA, QA + 256))
    for r in range(3):
        p += list(range(QB + r * 64, QB + r * 64 + 64)) + list(range(QB + (3 + r) * 64, QB + (3 + r) * 64 + 64))
    p += list(range(KA, KA + 256))
    p += list(range(KB, KB + 128))
    for r in range(3):
        p += list(range(QC + r * 64, QC + r * 64 + 64)) + list(range(QC + (3 + r) * 64, QC + (3 + r) * 64 + 64))
    p += list(range(KC, KC + 128))
    p += list(range(VA, VA + 256)) + list(range(VB, VB + 128)) + list(range(VC, VC + 128))
    assert len(p) == 2048 and len(set(p)) == 2048
    return np.array(p)


def _t5_bucket_idx():
    import jax
    import jax.numpy as jnp
    cpu = jax.devices("cpu")[0]
    with jax.default_device(cpu):
        rel = jnp.arange(-383, 384)
        nb = 16
        ret = (rel > 0).astype(jnp.int32) * nb
        n = jnp.abs(rel)
        max_exact = nb // 2
        nf = jnp.maximum(n, max_exact).astype(jnp.float32)
        large = max_exact + (jnp.log(nf / max_exact) / math.log(128 / max_exact) * (nb - max_exact)).astype(jnp.int32)
        large = jnp.minimum(large, nb - 1)
        out = ret + jnp.where(n < max_exact, n, large)
        return np.asarray(out)


def _na_tables(rpb, q):
    out = np.empty((5, 128, 4, 7, 128), np.float32)
    k = np.arange(128)[:, None, None]
    kb = np.arange(7)[None, :, None]
    qq = np.arange(128)[None, None, :]
    for v, j in enumerate((0, 1, 2, 30, 31)):
        n = q * 32 + j
        if v == 2:
            n = 64
        m = n - 3 + kb
        kr = 2 * m + k // 64
        kc = (k % 64) + 0 * kb + 0 * qq
        qr = 2 * n + qq // 64
        qc = qq % 64
        rs = np.clip(qr - 4, 0, 256 - 8)
        cs = np.clip(qc - 8, 0, 64 - 16)
        valid = (m >= 0) & (m < 128) & (kr >= rs) & (kr < rs + 8) & (kc >= cs) & (kc < cs + 16)
        dr = np.clip(kr - qr + 7, 0, 14) + 0 * kc
        dc = np.clip(kc - qc + 15, 0, 30) + 0 * kr
        valid = np.broadcast_to(valid, dr.shape)
        for h in range(4):
            out[v, :, h] = np.where(valid, rpb[h][dr, dc], np.float32(MASKV))
    return out


def _sw_tables(t5, q, bidx):
    out = np.empty((5, 128, 2, 3, 128), np.float32)
    k = np.arange(128)[:, None]
    qq = np.arange(128)[None, :]
    for t in range(3):
        rel = (t - 1) * 128 + k - qq
        valid = np.abs(rel) <= 128
        b = bidx[rel + 383]
        for g in range(2):
            for r in range(3):
                out[t, :, g, r, :] = np.where(valid, t5[b, g * 3 + r], np.float32(MASKV))
    out[3] = out[0] if q != 0 else np.float32(MASKV)
    out[4] = out[2] if q != 3 else np.float32(MASKV)
    return out.reshape(5, 128, 2, 384)


def _rope_tables(q):
    t = np.arange(T) + q * T
    row = (t // 64).astype(np.float32)
    col = (t % 64).astype(np.float32)
    freqs = (np.float32(10000.0) ** (-np.arange(0, 32, 2, dtype=np.float32) / np.float32(32))).astype(np.float32)
    ang = np.stack([row[:, None] * freqs, col[:, None] * freqs], axis=1).astype(np.float32)
    return np.cos(ang).astype(np.float32).reshape(T, 32), np.sin(ang).astype(np.float32).reshape(T, 32)


def _idx_table(q):
    p = np.arange(128)
    idx = np.zeros((128, 12), np.int32)
    lq, rq = (q - 1) % 4, (q + 1) % 4
    for c in range(3):
        idx[:, c] = lq * 768 + 384 + c * 128 + p
        idx[:, 3 + c] = rq * 768 + 0 + c * 128 + p
        idx[:, 6 + c] = lq * 768 + 384 + c * 128 + p
        idx[:, 9 + c] = rq * 768 + 0 + c * 128 + p
    return idx


def build(stop_after=None, dbg=None):
    nc = bass.Bass("TRN2", target_bir_lowering=False)
    _uid = [0]

    def U(n):
        _uid[0] += 1
        return "%s_%d" % (n, _uid[0])

    def din(name, shape, dt=F32):
        return nc.dram_tensor(name, list(shape), dt, kind="ExternalInput").ap()

    x_in = din("x_in", [T, D])
    c_t = din("c_t", [128, 8])
    w_mod = din("w_mod", [L, D, 6 * D])
    b_mod = din("b_mod", [L, 6 * D])
    g_attn_t = din("g_attn_t", [L, 128, 8])
    g_ffn_t = din("g_ffn_t", [L, 128, 8])
    w_in = din("w_in", [L, D, 2048])
    w_o = din("w_o", [L, D, D])
    w_gu = din("w_gu", [L, D, 2 * F])
    w_down = din("w_down", [L, F, D])
    g_group = din("g_group", [L, D])
    gqk = din("gqk", [L, 512])
    g_final = din("g_final", [1, D])
    sink = din("sink", [L, 6])
    rope_c = din("rope_c", [T, 32])
    rope_s = din("rope_s", [T, 32])
    na_tab = din("na_tab", [L, 5, 128, 4 * 7 * 128])
    sw_tab = din("sw_tab", [5, 128, 2 * 384])
    idx_in = din("idx_in", [128, 12], I32)
    idb_in = din("idb_in", [128, 128], BF16)
    idf_in = din("idf_in", [128, 128], F32)
    out = nc.dram_tensor("out", [T, D], F32, kind="ExternalOutput").ap()
    dbg_outs = {}
    for (dn, dshape, ddt) in (dbg or []):
        dbg_outs[dn] = nc.dram_tensor("dbg_" + dn, list(dshape), ddt, kind="ExternalOutput").ap()

    def dscr(name, shape, dt):
        return nc.dram_tensor(name, list(shape), dt)

    modrow = dscr("modrow", [L, 6 * D], F32).ap()
    x1 = dscr("x1", [T, D], F32).ap()
    x2 = dscr("x2", [T, D], F32).ap()
    q_na = dscr("q_na", [2, 128, T], BF16).ap()
    q_sw = dscr("q_sw", [128, NB, 3, 128], BF16).ap()
    q_ax = dscr("q_ax", [3, 128, T], BF16).ap()
    ktns = dscr("ktns", [3, 128, T], BF16).ap()
    vns = dscr("vns", [T, 390], BF16).ap()
    ktax_loc = dscr("ktax_loc", [128, T], BF16)
    ktax_all = dscr("ktax_all", [4 * 128, T], BF16)
    vax_loc = [dscr("vax_loc%d" % i, [T // 2, 130], BF16) for i in range(2)]
    vax_all = [dscr("vax_all%d" % i, [4 * T // 2, 130], BF16) for i in range(2)]
    hk_loc = dscr("hk_loc", [768, 384], BF16)
    hk_all = dscr("hk_all", [4 * 768, 384], BF16)
    hv_loc = dscr("hv_loc", [768, 390], BF16)
    hv_all = dscr("hv_all", [4 * 768, 390], BF16)

    with ExitStack() as top:
        S = Sched(nc, top)
        R = {n: S.res(n, dma=True) for n in
             ("modrow", "x1", "x2", "qscr", "kvscr", "halo", "out", "dbg")}
        R_in = Res("inputs")
        R_gath = S.res("gathered")

        def phase_mod():
            with ExitStack() as st:
                sb = lambda n, s, d: st.enter_context(nc.sbuf_tensor(U(n), s, d))
                ct = sb("ct", [128, 8], F32); r_ct = S.res("ct", dma=True)
                bm = sb("bm", [1, L * 6 * D], F32); r_bm = S.res("bm", dma=True)
                ms = sb("ms", [1, L * 6 * D], F32); r_ms = S.res("ms")
                wm = [sb("wm%d" % i, [128, 8, 512], F32) for i in range(2)]
                r_wm = [S.res("wm%d" % i, dma=True) for i in range(2)]
                pm = [st.enter_context(nc.psum_tensor(U("pm%d" % i), [128, 512], F32)) for i in range(2)]
                r_pm = [S.res("pm%d" % i) for i in range(2)]
                S.dma("sp", ct[:], c_t, [R_in], [r_ct])
                S.dma("sp", bm[:], b_mod.rearrange("l n -> (l n)").rearrange("(o n) -> o n", o=1), [R_in], [r_bm])
                S.op("act", lambda E: E.activation(out=ct[:], in_=ct[:], func=AF.Silu), [r_ct], [r_ct])
                it = 0
                for l in range(L):
                    for j in range(12):
                        s = it % 2
                        S.dma("sp", wm[s][:], w_mod[l, :, j * 512:(j + 1) * 512].rearrange("(kc p) n -> p kc n", p=128),
                              [R_in], [r_wm[s]])
                        for kc in range(8):
                            S.op("pe", lambda E, kc=kc, s=s: E.matmul(pm[s][0:1, :], lhsT=ct[:, kc:kc + 1], rhs=wm[s][:, kc, :],
                                                                     start=(kc == 0), stop=(kc == 7)),
                                 [r_ct, r_wm[s]], [r_pm[s]])
                        o = l * 6 * D + j * 512
                        S.op("dve", lambda E, s=s, o=o: E.tensor_tensor(out=ms[0:1, o:o + 512], in0=pm[s][0:1, :],
                                                                       in1=bm[0:1, o:o + 512], op=ALU.add),
                             [r_pm[s], r_bm], [r_ms])
                        it += 1
                S.dma("sp", modrow.rearrange("l n -> (l n)").rearrange("(o n) -> o n", o=1), ms[:], [r_ms], [R["modrow"]])
                S.barrier()

        def norm_ops(xt, r_xt, nblk, ssq, r_ssq, junk, r_junk, xn, r_xn):
            ops = []
            for b in range(nblk):
                if junk is None:
                    ops.append(lambda b=b: S.op("act", lambda E: E.activation(out=xn[:, b, :], in_=xt[b], func=AF.Square,
                                                                              accum_out=ssq[:, b:b + 1]),
                                                [r_xt[b]], [r_xn[b], r_ssq]))
                else:
                    ops.append(lambda b=b: S.op("act", lambda E: E.activation(out=junk[:], in_=xt[b], func=AF.Square,
                                                                              accum_out=ssq[:, b:b + 1]),
                                                [r_xt[b]], [r_junk, r_ssq]))
            ops.append(lambda: S.op("dve", lambda E: E.tensor_scalar(out=ssq[:, 4:4 + nblk], in0=ssq[:, 0:nblk], scalar1=1.0 / D, scalar2=EPS,
                                                                     op0=ALU.mult, op1=ALU.add), [r_ssq], [r_ssq]))
            ops.append(lambda: S.op("act", lambda E: E.activation(out=ssq[:, 4:4 + nblk], in_=ssq[:, 4:4 + nblk], func=AF.Sqrt), [r_ssq], [r_ssq]))
            ops.append(lambda: S.op("dve", lambda E: E.reciprocal(out=ssq[:, 8:8 + nblk], in_=ssq[:, 4:4 + nblk]), [r_ssq], [r_ssq]))
            for b in range(nblk):
                if b % 2 == 0:
                    ops.append(lambda b=b: S.op("dve", lambda E: E.tensor_scalar(out=xn[:, b, :], in0=xt[b], scalar1=ssq[:, 8 + b:9 + b],
                                                                                 scalar2=None, op0=ALU.mult), [r_xt[b], r_ssq], [r_xn[b]]))
                else:
                    ops.append(lambda b=b: S.op("act", lambda E: E.activation(out=xn[:, b, :], in_=xt[b], func=AF.Identity,
                                                                              scale=ssq[:, 8 + b:9 + b]), [r_xt[b], r_ssq], [r_xn[b]]))
            return ops

        def norm_part(xt, r_xt, nblk, ssq, r_ssq, junk, r_junk, xn, r_xn):
            for f in norm_ops(xt, r_xt, nblk, ssq, r_ssq, junk, r_junk, xn, r_xn):
                f()

        def tr_part(nblk, xn, r_xn, psT, r_psT, hT, r_hT, a1, a0, r_a, idb, r_idb):
            for kc in range(8):
                s = kc % 2
                for b in range(nblk):
                    S.op("pe", lambda E, b=b, kc=kc, s=s: E.transpose(psT[s][:, b * 128:(b + 1) * 128],
                                                                     xn[:, b, kc * 128:(kc + 1) * 128], idb[:]),
                         [r_xn[b], r_idb], [r_psT[s]])
                if kc % 2 == 0:
                    S.op("act", lambda E, kc=kc, s=s: E.activation(out=hT[:, kc, 0:nblk * 128], in_=psT[s][:, 0:nblk * 128],
                                                                   func=AF.Identity, scale=a1[:, kc:kc + 1], bias=a0[:, kc:kc + 1]),
                         [r_psT[s], r_a], [r_hT])
                else:
                    S.op("dve", lambda E, kc=kc, s=s: E.tensor_scalar(out=hT[:, kc, 0:nblk * 128], in0=psT[s][:, 0:nblk * 128],
                                                                      scalar1=a1[:, kc:kc + 1], scalar2=a0[:, kc:kc + 1],
                                                                      op0=ALU.mult, op1=ALU.add),
                         [r_psT[s], r_a], [r_hT])

        def norm_transpose(xt, r_xt, nblk, ssq, r_ssq, junk, r_junk, xn, r_xn, psT, r_psT, hT, r_hT, a1, a0, r_a, idb, r_idb):
            norm_part(xt, r_xt, nblk, ssq, r_ssq, junk, r_junk, xn, r_xn)
            tr_part(nblk, xn, r_xn, psT, r_psT, hT, r_hT, a1, a0, r_a, idb, r_idb)

        def load_mod_cols(st, l, off_sc, off_sh, g_t, name):
            sb = lambda n, s, d: st.enter_context(nc.sbuf_tensor(U(n), s, d))
            a1 = sb(name + "a1", [128, 8], F32)
            a0 = sb(name + "a0", [128, 8], F32)
            gt = sb(name + "gt", [128, 8], F32)
            r_a = S.res(name + "a", dma=True)
            S.dma("sp", a1[:], modrow[l, off_sc:off_sc + D].rearrange("(c p) -> p c", p=128), [R["modrow"]], [r_a],
                  allow_slow_non_contiguous=True)
            S.dma("sp", a0[:], modrow[l, off_sh:off_sh + D].rearrange("(c p) -> p c", p=128), [R["modrow"]], [r_a],
                  allow_slow_non_contiguous=True)
            S.dma("sp", gt[:], g_t[l], [R_in], [r_a])
            S.op("dve", lambda E: E.tensor_scalar(out=a1[:], in0=a1[:], scalar1=1.0, scalar2=None, op0=ALU.add), [r_a], [r_a])
            S.op("dve", lambda E: E.tensor_tensor(out=a1[:], in0=a1[:], in1=gt[:], op=ALU.mult), [r_a], [r_a])
            return a1, a0, r_a

        def phase_a(l, x_src, r_xsrc):
            with ExitStack() as st:
                sb = lambda n, s, d: st.enter_context(nc.sbuf_tensor(U(n), s, d))
                ps = lambda n, s, d: st.enter_context(nc.psum_tensor(U(n), s, d))
                win = sb("win", [128, 8, 2048], BF16); r_win = S.res("win", dma=True)
                idb = sb("idb", [128, 128], BF16); r_idb = S.res("idb", dma=True)
                gq = sb("gq", [128, 512], F32); r_gq = S.res("gq", dma=True)
                a1, a0, r_a = load_mod_cols(st, l, D, 0, g_attn_t, "A")
                xt = [sb("xt%d" % i, [128, 4, D], F32) for i in range(2)]
                r_xt = [[S.res("xt%d_%d" % (i, b), dma=True) for b in range(4)] for i in range(2)]
                rc = [sb("rc%d" % i, [128, 4, 32], F32) for i in range(2)]
                rs = [sb("rs%d" % i, [128, 4, 32], F32) for i in range(2)]
                r_rope = [S.res("rope%d" % i, dma=True) for i in range(2)]
                ssq2 = [sb("ssq%d" % i, [128, 12], F32) for i in range(2)]; r_ssq2 = [S.res("ssq%d" % i) for i in range(2)]
                junk = sb("junk", [128, D], BF16); r_junk = S.res("junk")
                xn2 = [sb("xn%d" % i, [128, 4, D], BF16) for i in range(2)]
                r_xn2 = [[S.res("xn%d_%d" % (i, b)) for b in range(4)] for i in range(2)]
                hT2 = [sb("hT%d" % i, [128, 8, 512], BF16) for i in range(2)]; r_hT2 = [S.res("hT%d" % i) for i in range(2)]
                fm = [sb("fm%d" % i, [128, 8, 512], BF16) for i in range(2)]; r_fm = [S.res("fm%d" % i) for i in range(2)]
                axs = [sb("axs%d" % i, [128, 4, 512], BF16) for i in range(2)]; r_axs = [S.res("axs%d" % i) for i in range(2)]
                vst = [sb("vst%d" % i, [128, 4, 8, 65], BF16) for i in range(2)]; r_vst = [S.res("vst%d" % i) for i in range(2)]
                axr2 = [sb("axr%d" % i, [128, 512], F32) for i in range(2)]; r_axr2 = [S.res("axr%d" % i) for i in range(2)]
                axq2 = [sb("axq%d" % i, [128, 512], F32) for i in range(2)]; r_axq2 = [S.res("axq%d" % i) for i in range(2)]
                axg2 = [sb("axg%d" % i, [128, 512], F32) for i in range(2)]; r_axg2 = [S.res("axg%d" % i) for i in range(2)]
                axt2 = [[sb("axt%d_%d" % (k, i), [128, 256], F32) for i in range(4)] for k in range(2)]
                r_axt2 = [[S.res("axt%d_%d" % (k, i)) for i in range(4)] for k in range(2)]
                axo2 = [sb("axo%d" % i, [128, 512], BF16) for i in range(2)]; r_axo2 = [S.res("axo%d" % i) for i in range(2)]
                s82 = [sb("s8_%d" % i, [128, 24], F32) for i in range(2)]; r_s82 = [S.res("s8_%d" % i) for i in range(2)]
                psT = [ps("psT%d" % i, [128, 1024], BF16) for i in range(2)]; r_psT = [S.res("psT%d" % i) for i in range(2)]
                pfm = [ps("pfm%d" % i, [128, 512], F32) for i in range(2)]; r_pfm = [S.res("pfm%d" % i) for i in range(2)]
                pax2 = [ps("pax%d" % i, [128, 512], F32) for i in range(2)]; r_pax2 = [S.res("pax%d" % i) for i in range(2)]
                pv = ps("pv", [128, 512], F32); r_pv = S.res("pv")
                paT = ps("paT", [128, 1024], BF16); r_paT = S.res("paT")

                for kc in range(8):
                    S.dma("pool", win[:, kc, :], w_in[l, kc * 128:(kc + 1) * 128, :], [R_in], [r_win])
                S.dma("sp", idb[:], idb_in, [R_in], [r_idb])
                S.dma("sp", gq[:], gqk[l:l + 1, :].partition_broadcast(128), [R_in], [r_gq])
                S.op("dve", lambda E: E.tensor_scalar(out=gq[:, 0:384], in0=gq[:, 0:384], scalar1=0.125, scalar2=None,
                                                      op0=ALU.mult), [r_gq], [r_gq])
                for i in range(2):
                    S.op("pool", lambda E, i=i: E.memset(vst[i][:], 1.0), [], [r_vst[i]])

                def load_group(G):
                    s = G % 2
                    for b in range(4):
                        S.dma("sp", xt[s][:, b, :], x_src[(G * 4 + b) * 128:(G * 4 + b + 1) * 128, :], [r_xsrc], [r_xt[s][b]])
                    S.dma("sp", rc[s][:], rope_c[G * 512:(G + 1) * 512, :].rearrange("(b p) c -> p b c", p=128), [R_in], [r_rope[s]])
                    S.dma("sp", rs[s][:], rope_s[G * 512:(G + 1) * 512, :].rearrange("(b p) c -> p b c", p=128), [R_in], [r_rope[s]])

                def do_norm(G):
                    k = G % 2
                    norm_part([xt[k][:, b, :] for b in range(4)], r_xt[k], 4, ssq2[k], r_ssq2[k], junk, r_junk, xn2[k], r_xn2[k])

                def do_tr(G):
                    k = G % 2
                    tr_part(4, xn2[k], r_xn2[k], psT, r_psT, hT2[k], r_hT2[k], a1, a0, r_a, idb, r_idb)

                load_group(0)
                do_norm(0)
                do_tr(0)
                for G in range(NG):
                    s = G % 2
                    hT, r_hT = hT2[s], r_hT2[s]
                    if G + 1 < NG:
                        load_group(G + 1)
                    for c in range(8):
                        p = c % 2
                        for kc in range(8):
                            S.op("pe", lambda E, c=c, kc=kc, p=p: E.matmul(pfm[p][:], lhsT=win[:, kc, c * 128:(c + 1) * 128],
                                                                          rhs=hT[:, kc, :], start=(kc == 0), stop=(kc == 7)),
                                 [r_win, r_hT], [r_pfm[p]])
                        sc = 0.125 if c < 5 else 1.0
                        if c % 2 == 0:
                            S.op("act", lambda E, c=c, p=p, sc=sc: E.activation(out=fm[s][:, c, :], in_=pfm[p][:], func=AF.Copy, scale=sc),
                                 [r_pfm[p]], [r_fm[s]])
                        else:
                            S.op("dve", lambda E, c=c, p=p, sc=sc: E.tensor_scalar(out=fm[s][:, c, :], in0=pfm[p][:], scalar1=sc,
                                                                                  scalar2=None, op0=ALU.mult),
                                 [r_pfm[p]], [r_fm[s]])
                    def MM(b):
                        pax, r_pax = pax2[b % 2], r_pax2[b % 2]
                        for kc in range(8):
                            S.op("pe", lambda E, kc=kc: E.matmul(pax[:], lhsT=hT[:, kc, b * 128:(b + 1) * 128],
                                                                rhs=win[:, kc, 1024:1536], start=(kc == 0), stop=(kc == 7)),
                                 [r_win, r_hT], [r_pax])
                        for kc in range(8):
                            S.op("pe", lambda E, kc=kc: E.matmul(pv[:], lhsT=hT[:, kc, b * 128:(b + 1) * 128],
                                                                rhs=win[:, kc, 1536:2048], start=(kc == 0), stop=(kc == 7)),
                                 [r_win, r_hT], [r_pv])
                        S.op("act", lambda E: E.activation(out=vst[s][:, b, :, 0:64],
                                                           in_=pv[:].rearrange("p (h d) -> p h d", d=64), func=AF.Copy),
                             [r_pv], [r_vst[s]])

                    def chain(b):
                        k2 = b % 2
                        pax, r_pax = pax2[k2], r_pax2[k2]
                        axr, r_axr, axq, r_axq, axg, r_axg = axr2[k2], r_axr2[k2], axq2[k2], r_axq2[k2], axg2[k2], r_axg2[k2]
                        axt, r_axt, axo, r_axo, s8, r_s8 = axt2[k2], r_axt2[k2], axo2[k2], r_axo2[k2], s82[k2], r_s82[k2]
                        xv = axg[:].rearrange("p (h a t f) -> p h a t f", h=8, a=2, t=2)
                        ov = axo[:].rearrange("p (h a t f) -> p h a t f", h=8, a=2, t=2)
                        x1v, x2v = xv[:, :, :, 0, :], xv[:, :, :, 1, :]
                        cv = rc[s][:, b, :].rearrange("p (o a f) -> p o a f", o=1, a=2).broadcast_to([128, 8, 2, 16])
                        sv = rs[s][:, b, :].rearrange("p (o a f) -> p o a f", o=1, a=2).broadcast_to([128, 8, 2, 16])
                        tv = [axt[i][:].rearrange("p (h a f) -> p h a f", h=8, a=2) for i in range(4)]
                        h3 = lambda t: t[:].rearrange("p (h d) -> p h d", d=64)
                        return [
                            lambda: S.op("act", lambda E: E.activation(out=axr[:], in_=pax[:], func=AF.Copy), [r_pax], [r_axr]),
                            lambda: S.op("pool", lambda E: E.tensor_tensor(out=axq[:], in0=axr[:], in1=axr[:], op=ALU.mult), [r_axr], [r_axq]),
                            lambda: S.op("dve", lambda E: E.tensor_reduce(out=s8[:, 0:8], in_=h3(axq), axis=AXL.X, op=ALU.add), [r_axq], [r_s8]),
                            lambda: S.op("dve", lambda E: E.tensor_scalar(out=s8[:, 8:16], in0=s8[:, 0:8], scalar1=1.0 / 64, scalar2=EPS,
                                                                          op0=ALU.mult, op1=ALU.add), [r_s8], [r_s8]),
                            lambda: S.op("act", lambda E: E.activation(out=s8[:, 8:16], in_=s8[:, 8:16], func=AF.Sqrt), [r_s8], [r_s8]),
                            lambda: S.op("dve", lambda E: E.reciprocal(out=s8[:, 16:24], in_=s8[:, 8:16]), [r_s8], [r_s8]),
                            lambda: S.op("pool", lambda E: E.tensor_tensor(
                                out=h3(axg), in0=h3(axr),
                                in1=s8[:, 16:24].rearrange("p (h o) -> p h o", o=1).broadcast_to([128, 8, 64]), op=ALU.mult),
                                         [r_axr, r_s8], [r_axg]),
                            lambda: S.op("dve", lambda E: E.tensor_tensor(out=axg[:], in0=axg[:], in1=gq[:], op=ALU.mult), [r_axg, r_gq], [r_axg]),
                            lambda: S.op("dve", lambda E: E.tensor_tensor(out=tv[0], in0=x1v, in1=cv, op=ALU.mult), [r_axg, r_rope[s]], [r_axt[0]]),
                            lambda: S.op("pool", lambda E: E.tensor_tensor(out=tv[1], in0=x2v, in1=sv, op=ALU.mult), [r_axg, r_rope[s]], [r_axt[1]]),
                            lambda: S.op("dve", lambda E: E.tensor_tensor(out=tv[2], in0=x2v, in1=cv, op=ALU.mult), [r_axg, r_rope[s]], [r_axt[2]]),
                            lambda: S.op("pool", lambda E: E.tensor_tensor(out=tv[3], in0=x1v, in1=sv, op=ALU.mult), [r_axg, r_rope[s]], [r_axt[3]]),
                            lambda: S.op("dve", lambda E: E.tensor_tensor(out=ov[:, :, :, 0, :], in0=tv[0], in1=tv[1], op=ALU.subtract),
                                         [r_axt[0], r_axt[1]], [r_axo]),
                            lambda: S.op("pool", lambda E: E.tensor_tensor(out=ov[:, :, :, 1, :], in0=tv[2], in1=tv[3], op=ALU.add),
                                         [r_axt[2], r_axt[3]], [r_axo]),
                        ]

                    def TR(b):
                        axo, r_axo = axo2[b % 2], r_axo2[b % 2]
                        for j in range(4):
                            S.op("pe", lambda E, j=j: E.transpose(paT[:, j * 128:(j + 1) * 128], axo[:, j * 128:(j + 1) * 128], idb[:]),
                                 [r_axo, r_idb], [r_paT])
                        S.op("act", lambda E: E.activation(out=axs[s][:, :, b * 128:(b + 1) * 128],
                                                           in_=paT[:, 0:512].rearrange("p (j t) -> p j t", j=4), func=AF.Copy),
                             [r_paT], [r_axs[s]])

                    def run_chains(b0, b1):
                        c0, c1 = chain(b0), chain(b1)
                        for f0, f1 in zip(c0, c1):
                            f0()
                            f1()

                    MM(0)
                    MM(1)
                    run_chains(0, 1)
                    MM(2)
                    MM(3)
                    TR(0)
                    TR(1)
                    if G + 1 < NG:
                        do_norm(G + 1)
                    run_chains(2, 3)
                    if G + 1 < NG:
                        do_tr(G + 1)
                    TR(2)
                    TR(3)
                    gsl = slice(G * 512, (G + 1) * 512)
                    for c in range(2):
                        S.dma("sp", q_na[c, :, gsl], fm[s][:, c, :], [r_fm[s]], [R["qscr"]])
                    for r in range(3):
                        S.dma("sp", q_sw[:, G * 4:(G + 1) * 4, r, :], fm[s][:, 2 + r, :].rearrange("p (b t) -> p b t", b=4),
                              [r_fm[s]], [R["qscr"]])
                    for c in range(3):
                        S.dma("sp", ktns[c, :, gsl], fm[s][:, 5 + c, :], [r_fm[s]], [R["kvscr"]])
                    for r in range(3):
                        S.dma("sp", q_ax[r, :, gsl], axs[s][:, r, :], [r_axs[s]], [R["qscr"]])
                    S.dma("sp", ktax_loc.ap()[:, gsl], axs[s][:, 3, :], [r_axs[s]], [R["kvscr"]])
                    S.dma("sp", vns[G * 512:(G + 1) * 512, :].rearrange("(b p) c -> p b c", p=128),
                          vst[s][:, :, 0:6, :].rearrange("p b h d -> p b (h d)"), [r_vst[s]], [R["kvscr"]])
                    S.dma("sp", vax_loc[G // 4].ap()[(G % 4) * 512:(G % 4 + 1) * 512, :].rearrange("(b p) c -> p b c", p=128),
                          vst[s][:, :, 6:8, :].rearrange("p b h d -> p b (h d)"), [r_vst[s]], [R["kvscr"]])
                    if G == 0 or G == NG - 1:
                        side = 0 if G == 0 else 1
                        tsl = slice(0, 384) if G == 0 else slice(128, 512)
                        bsl = slice(0, 3) if G == 0 else slice(1, 4)
                        for c in range(3):
                            S.dma("sp", hk_loc.ap()[side * 384 + c * 128: side * 384 + (c + 1) * 128, :], fm[s][:, 5 + c, tsl],
                                  [r_fm[s]], [R["halo"]])
                        S.dma("sp", hv_loc.ap()[side * 384:(side + 1) * 384, :].rearrange("(b p) c -> p b c", p=128),
                              vst[s][:, bsl, 0:6, :].rearrange("p b h d -> p b (h d)"), [r_vst[s]], [R["halo"]])
                S.barrier()

        def exchange():
            for (a, b_) in ((ktax_loc, ktax_all), (vax_loc[0], vax_all[0]), (vax_loc[1], vax_all[1]), (hk_loc, hk_all), (hv_loc, hv_all)):
                cc = nc.gpsimd.collective_compute("AllGather", ALU.bypass, replica_groups=[[0, 1, 2, 3], [4, 5, 6, 7]],
                                                  ins=[a.ap().opt()], outs=[b_.ap().opt()])
                cc.then_inc(S.sems["pool"], 1)
                S.cnt["pool"] += 1
                nc.gpsimd.wait_ge(S.sems["pool"], S.cnt["pool"])
            R_gath.w = ("pool", S.cnt["pool"])
            R_gath.r = {}
            S.barrier()

        def phase_b(l, x_src, r_xsrc):
            with ExitStack() as st:
                sb = lambda n, s, d: st.enter_context(nc.sbuf_tensor(U(n), s, d))
                ps = lambda n, s, d: st.enter_context(nc.psum_tensor(U(n), s, d))
                ktax = sb("ktax", [128, 4 * T], BF16); r_ktax = S.res("ktax", dma=True)
                vax = sb("vax", [128, 128, 130], BF16); r_vax = S.res("vax", dma=True)
                wo = sb("wo", [128, 8, D], BF16); r_wo = S.res("wo", dma=True)
                gta = sb("gta", [128, D], F32); ggr = sb("ggr", [128, D], F32); r_bc = S.res("bcB", dma=True)
                esk = sb("esk", [128, 6], F32); r_esk = S.res("esk", dma=True)
                nab = sb("nab", [128, 4 * 7 * 128], BF16); r_nab = S.res("nab", dma=True)
                swb = sb("swb", [128, 5, 768], BF16); r_swb = S.res("swb", dma=True)
                idx = sb("idx", [128, 12], I32); r_idx = S.res("idx", dma=True)
                idb = sb("idb", [128, 128], BF16); idf = sb("idf", [128, 128], F32); r_id = S.res("idB", dma=True)
                hkL = sb("hkL", [128, 3, 384], BF16); hkR = sb("hkR", [128, 3, 384], BF16)
                hvL = sb("hvL", [128, 3, 390], BF16); hvR = sb("hvR", [128, 3, 390], BF16); r_halo = S.res("haloB", dma=True)
                ktw2 = sb("ktw2", [128, 3, 10 * 128], BF16)
                vw = sb("vw", [128, 10, 390], BF16); r_win = S.res("winB", dma=True)
                qna = [sb("qna%d" % i, [128, 2, 512], BF16) for i in range(2)]
                qsw = [sb("qsw%d" % i, [128, 4, 384], BF16) for i in range(2)]
                qax = [sb("qax%d" % i, [128, 3, 512], BF16) for i in range(2)]
                r_q = [S.res("q%d" % i, dma=True) for i in range(2)]
                xr = [sb("xr%d" % i, [128, D], F32) for i in range(2)]; r_xr = [S.res("xrB%d" % i, dma=True) for i in range(2)]
                pt = [sb("pt%d" % i, [128, 1024], BF16) for i in range(3)]; r_pt = [S.res("pt%d" % i) for i in range(3)]
                otk = sb("otk", [128, 4, 16, 65], F32); r_otk = [S.res("otk%d" % b) for b in range(4)]
                oTs = [sb("oTs%d" % i, [65, 512], F32) for i in range(2)]; r_oTs = [S.res("oTs%d" % i) for i in range(2)]
                den2 = [sb("den%d" % i, [128, 48], F32) for i in range(2)]; r_den2 = [S.res("den%d" % i) for i in range(2)]
                onr2 = [sb("onr%d" % i, [128, D], F32) for i in range(2)]; r_onr2 = [S.res("onr%d" % i) for i in range(2)]
                yb2 = [sb("yb%d" % i, [128, D], F32) for i in range(2)]; r_yb2 = [S.res("yb%d" % i) for i in range(2)]
                yT2 = [sb("yT%d" % i, [128, 8, 128], BF16) for i in range(2)]; r_yT2 = [S.res("yT%d" % i) for i in range(2)]
                tmp2 = [sb("tmpB%d" % i, [128, D], F32) for i in range(2)]; r_tmp2 = [S.res("tmpB%d" % i) for i in range(2)]
                pS = [ps("pS%d" % i, [128, 1024], F32) for i in range(2)]; r_pS = [S.res("pS%d" % i) for i in range(2)]
                pO = [ps("pO%d" % i, [128, 512], F32) for i in range(2)]; r_pO = [S.res("pO%d" % i) for i in range(2)]
                pna = ps("pna", [128, 4, 65], F32); r_pna = S.res("pna")
                psw = ps("psw", [128, 6, 65], F32); r_psw = S.res("psw")

                for kc in range(8):
                    S.dma("pool", wo[:, kc, :], w_o[l, kc * 128:(kc + 1) * 128, :], [R_in], [r_wo])
                S.dma("sp", gta[:], modrow[l:l + 1, 2 * D:3 * D].partition_broadcast(128), [R["modrow"]], [r_bc])
                S.dma("sp", ggr[:], g_group[l:l + 1, :].partition_broadcast(128), [R_in], [r_bc])
                S.dma("sp", esk[:], sink[l:l + 1, :].partition_broadcast(128), [R_in], [r_esk])
                S.op("act", lambda E: E.activation(out=esk[:], in_=esk[:], func=AF.Exp), [r_esk], [r_esk])
                cur_var = [0]
                S.dma("pool", nab[:], na_tab[l, 0], [R_in], [r_nab])
                for t in range(5):
                    S.dma("pool", swb[:, t, :], sw_tab[t], [R_in], [r_swb])
                S.dma("sp", idx[:], idx_in, [R_in], [r_idx])
                S.dma("sp", idb[:], idb_in, [R_in], [r_id])
                S.dma("sp", idf[:], idf_in, [R_in], [r_id])
                for c in range(3):
                    S.dma("pool", hkL[:, c, :], hk_all.ap()[:, :], [R_gath, r_idx], [r_halo],
                          indirect=bass.IndirectOffsetOnAxis(ap=idx[:, c:c + 1], axis=0))
                    S.dma("pool", hkR[:, c, :], hk_all.ap()[:, :], [R_gath, r_idx], [r_halo],
                          indirect=bass.IndirectOffsetOnAxis(ap=idx[:, 3 + c:4 + c], axis=0))
                    S.dma("pool", hvL[:, c, :], hv_all.ap()[:, :], [R_gath, r_idx], [r_halo],
                          indirect=bass.IndirectOffsetOnAxis(ap=idx[:, 6 + c:7 + c], axis=0))
                    S.dma("pool", hvR[:, c, :], hv_all.ap()[:, :], [R_gath, r_idx], [r_halo],
                          indirect=bass.IndirectOffsetOnAxis(ap=idx[:, 9 + c:10 + c], axis=0))

                def load_q(G):
                    s = G % 2
                    gsl = slice(G * 512, (G + 1) * 512)
                    S.dma("sp", qna[s][:], q_na[:, :, gsl].rearrange("c p t -> p c t"), [R["qscr"]], [r_q[s]])
                    S.dma("sp", qsw[s][:], q_sw[:, G * 4:(G + 1) * 4, :, :].rearrange("p b r t -> p b (r t)"), [R["qscr"]], [r_q[s]])
                    S.dma("sp", qax[s][:], q_ax[:, :, gsl].rearrange("c p t -> p c t"), [R["qscr"]], [r_q[s]])

                def load_win(G):
                    lo = max(4 * G - 3, 0)
                    hi = min(4 * G + 7, NB)
                    w0 = lo - (4 * G - 3)
                    n = hi - lo
                    S.dma("sp", ktw2[:, :, w0 * 128:(w0 + n) * 128], ktns[:, :, lo * 128:hi * 128].rearrange("c p t -> p c t"),
                          [R["kvscr"]], [r_win])
                    S.dma("sp", vw[:, w0:w0 + n, :], vns[lo * 128:hi * 128, :].rearrange("(b p) c -> p b c", p=128),
                          [R["kvscr"]], [r_win])

                def kt_blk(c, m, G):
                    if m < 0:
                        return hkL[:, c, (m + 3) * 128:(m + 4) * 128], r_halo
                    if m >= NB:
                        return hkR[:, c, (m - NB) * 128:(m - NB + 1) * 128], r_halo
                    w = m - (4 * G - 3)
                    return ktw2[:, c, w * 128:(w + 1) * 128], r_win

                def v_blk(m, G):
                    if m < 0:
                        return hvL[:, m + 3, :], r_halo
                    if m >= NB:
                        return hvR[:, m - NB, :], r_halo
                    return vw[:, m - (4 * G - 3), :], r_win

                load_q(0)
                load_win(0)
                for rk in range(4):
                    S.dma("sp", ktax[:, rk * T:(rk + 1) * T], ktax_all.ap()[rk * 128:(rk + 1) * 128, :], [R_gath], [r_ktax])
                for i in range(8):
                    S.dma("sp", vax[:, i * 16:(i + 1) * 16, :],
                          vax_all[i % 2].ap()[(i // 2) * 2048:(i // 2 + 1) * 2048, :].rearrange("(b p) c -> p b c", p=128), [R_gath], [r_vax])
                sp_i = [0]
                pt_i = [0]

                class Unit:
                    pass

                def na_unit(G, s, b, h, var, last_h):
                    n = 4 * G + b
                    u = Unit()
                    c, hp = h // 2, h % 2
                    sp = sp_i[0] % 2; sp_i[0] += 1
                    pi = pt_i[0] % 3; pt_i[0] += 1
                    kbs = list(range(7)) if var != 2 else list(range(1, 6))

                    def fS():
                        if h == 0 and cur_var[0] != var:
                            S.dma("pool", nab[:], na_tab[l, var], [R_in], [r_nab])
                            cur_var[0] = var
                        r_bt = r_nab
                        bt = nab[:].rearrange("p (h k q) -> p h k q", h=4, k=7)
                        for kb in kbs:
                            m = n - 3 + kb
                            kap, r_k = kt_blk(c, m, G)
                            S.op("pe", lambda E, kap=kap, kb=kb: E.matmul(
                                pS[sp][:, kb * 128:(kb + 1) * 128], lhsT=kap[hp * 64:(hp + 1) * 64, :],
                                rhs=qna[s][hp * 64:(hp + 1) * 64, c, b * 128:(b + 1) * 128], start=True, stop=False),
                                 [r_k, r_q[s]], [r_pS[sp]])
                            S.op("pe", lambda E, kb=kb, bt=bt: E.matmul(
                                pS[sp][:, kb * 128:(kb + 1) * 128], lhsT=idb[:], rhs=bt[:, h, kb, :], start=False, stop=True),
                                 [r_bt, r_id], [r_pS[sp]])

                    def fE():
                        S.op("act", lambda E: E.activation(out=pt[pi][:, kbs[0] * 128:(kbs[-1] + 1) * 128],
                                                           in_=pS[sp][:, kbs[0] * 128:(kbs[-1] + 1) * 128], func=AF.Exp),
                             [r_pS[sp]], [r_pt[pi]])

                    def fP():
                        for kb in kbs:
                            m = n - 3 + kb
                            vap, r_v = v_blk(m, G)
                            S.op("pe", lambda E, vap=vap, kb=kb: E.matmul(
                                pna[:, h, :], lhsT=pt[pi][:, kb * 128:(kb + 1) * 128], rhs=vap[:, h * 65:(h + 1) * 65],
                                start=(kb == kbs[0]), stop=(kb == kbs[-1])), [r_v, r_pt[pi]], [r_pna])
                        if last_h:
                            S.op("dve", lambda E: E.tensor_copy(out=otk[:, b, 0:4, :], in_=pna[:]), [r_pna], [r_otk[b]])
                    u.S, u.E, u.P = fS, fE, fP
                    return u

                def sw_unit(G, s, b, g, post):
                    n = 4 * G + b
                    u = Unit()
                    sp = sp_i[0] % 2; sp_i[0] += 1
                    pi = pt_i[0] % 3; pt_i[0] += 1
                    pi2 = pt_i[0] % 3; pt_i[0] += 1

                    def tgt_of(t):
                        if t == 2:
                            return pO[g][:, 0:384], r_pO[g]
                        return pS[sp][:, t * 512:t * 512 + 384], r_pS[sp]

                    def fS():
                        for t in range(3):
                            m = n - 1 + t
                            tv = t
                            if n == 0 and t == 0:
                                tv = 3
                            if n == NB - 1 and t == 2:
                                tv = 4
                            kap, r_k = kt_blk(2, m, G)
                            tgt, r_t = tgt_of(t)
                            S.op("pe", lambda E, kap=kap, tgt=tgt: E.matmul(
                                tgt, lhsT=kap[g * 64:(g + 1) * 64, :], rhs=qsw[s][g * 64:(g + 1) * 64, b, :], start=True, stop=False),
                                 [r_k, r_q[s]], [r_t])
                            S.op("pe", lambda E, tv=tv, tgt=tgt: E.matmul(
                                tgt, lhsT=idb[:], rhs=swb[:, tv, g * 384:(g + 1) * 384], start=False, stop=True),
                                 [r_swb, r_id], [r_t])

                    def fE():
                        S.op("act", lambda E: E.activation(
                            out=pt[pi][:, 0:768].rearrange("p (t q) -> p t q", t=2),
                            in_=pS[sp][:].rearrange("p (t q) -> p t q", t=2)[:, :, 0:384], func=AF.Exp),
                             [r_pS[sp]], [r_pt[pi]])
                        S.op("act", lambda E: E.activation(out=pt[pi2][:, 0:384], in_=pO[g][:, 0:384], func=AF.Exp),
                             [r_pO[g]], [r_pt[pi2]])

                    def fP():
                        for r in range(3):
                            for t in range(3):
                                m = n - 1 + t
                                vap, r_v = v_blk(m, G)
                                if t < 2:
                                    pap, r_p = pt[pi][:, t * 384 + r * 128: t * 384 + (r + 1) * 128], r_pt[pi]
                                else:
                                    pap, r_p = pt[pi2][:, r * 128:(r + 1) * 128], r_pt[pi2]
                                S.op("pe", lambda E, vap=vap, r=r, t=t, pap=pap: E.matmul(
                                    psw[:, g * 3 + r, :], lhsT=pap, rhs=vap[:, 260 + g * 65: 260 + (g + 1) * 65],
                                    start=(t == 0), stop=(t == 2)), [r_v, r_p], [r_psw])
                        if g == 1:
                            S.op("dve", lambda E: E.tensor_copy(out=otk[:, b, 4:10, :], in_=psw[:]), [r_psw], [r_otk[b]])
                        if post is not None:
                            post()
                    u.S, u.E, u.P = fS, fE, fP
                    return u

                def ax_unit(G, s, r, kb):
                    u = Unit()
                    sp = sp_i[0] % 2; sp_i[0] += 1
                    pi = pt_i[0] % 3; pt_i[0] += 1

                    def fS():
                        for g in range(2):
                            S.op("pe", lambda E, g=g: E.matmul(
                                pS[sp][:, g * 512:(g + 1) * 512], lhsT=ktax[g * 64:(g + 1) * 64, kb * 128:(kb + 1) * 128],
                                rhs=qax[s][g * 64:(g + 1) * 64, r, :], start=True, stop=True),
                                 [r_ktax, r_q[s]], [r_pS[sp]])

                    def fE():
                        S.op("act", lambda E: E.activation(out=pt[pi][:], in_=pS[sp][:], func=AF.Exp), [r_pS[sp]], [r_pt[pi]])

                    def fP():
                        for g in range(2):
                            S.op("pe", lambda E, g=g: E.matmul(
                                pO[g][0:65, :], lhsT=vax[:, kb, g * 65:(g + 1) * 65], rhs=pt[pi][:, g * 512:(g + 1) * 512],
                                start=(kb == 0), stop=(kb == 127)), [r_vax, r_pt[pi]], [r_pO[g]])
                        if kb == 127:
                            for g in range(2):
                                S.op("dve", lambda E, g=g: E.tensor_copy(out=oTs[g][:], in_=pO[g][0:65, :]), [r_pO[g]], [r_oTs[g]])
                            for g in range(2):
                                for b in range(4):
                                    S.op("pe", lambda E, b=b, g=g: E.transpose(pna[:, b, :], oTs[g][0:65, b * 128:(b + 1) * 128], idf[0:65, 0:65]),
                                         [r_oTs[g], r_id], [r_pna])
                                hh = 10 + g * 3 + r
                                S.op("dve", lambda E, hh=hh: E.tensor_copy(out=otk[:, :, hh, :], in_=pna[:]), [r_pna], r_otk)
                    u.S, u.E, u.P = fS, fE, fP
                    return u

                def nasw_units(Gq, blocks):
                    sq = Gq % 2
                    us = []
                    for b in blocks:
                        n = 4 * Gq + b
                        var = {0: 0, 1: 1, NB - 2: 3, NB - 1: 4}.get(n, 2)
                        for h in range(4):
                            us.append(na_unit(Gq, sq, b, h, var, h == 3))
                        for g in range(2):
                            post = None
                            if b == 3 and g == 1 and Gq + 1 < NG:
                                post = (lambda Gq=Gq: load_win(Gq + 1))
                            us.append(sw_unit(Gq, sq, b, g, post))
                    return us

                def emit(units):
                    if not units:
                        return
                    units[0].S()
                    if len(units) > 1:
                        units[1].S()
                    for i, u in enumerate(units):
                        u.E()
                        if i + 2 < len(units):
                            units[i + 2].S()
                        u.P()

                emit(nasw_units(0, [0, 1, 2, 3]))
                for G in range(NG):
                    s = G % 2
                    if G + 1 < NG:
                        load_q(G + 1)
                    units = []
                    for r in range(3):
                        for kb in range(128):
                            units.append(ax_unit(G, s, r, kb))
                    emit(units)
                    def comb_ops(b):
                        n = 4 * G + b
                        k = b % 2
                        dn, r_dn, on_, r_on, yb_, r_yb_, yT_, r_yT_, tm, r_tm = (den2[k], r_den2[k], onr2[k], r_onr2[k], yb2[k], r_yb2[k],
                                                                                 yT2[k], r_yT2[k], tmp2[k], r_tmp2[k])
                        xs = k
                        ops = []
                        ops.append(lambda: S.dma("sp", xr[xs][:], x_src[n * 128:(n + 1) * 128, :], [r_xsrc], [r_xr[xs]]))
                        ops.append(lambda: S.op("dve", lambda E: E.tensor_copy(out=dn[:, 0:16], in_=otk[:, b, :, 64]), [r_otk[b]], [r_dn]))
                        ops.append(lambda: S.op("dve", lambda E: E.tensor_tensor(out=dn[:, 4:10], in0=dn[:, 4:10], in1=esk[:], op=ALU.add),
                                                [r_dn, r_esk], [r_dn]))
                        ops.append(lambda: S.op("dve", lambda E: E.reciprocal(out=dn[:, 16:32], in_=dn[:, 0:16]), [r_dn], [r_dn]))
                        ops.append(lambda: S.op("pool", lambda E: E.tensor_tensor(
                            out=on_[:].rearrange("p (h d) -> p h d", d=64), in0=otk[:, b, :, 0:64],
                            in1=dn[:, 16:32].rearrange("p (h o) -> p h o", o=1).broadcast_to([128, 16, 64]), op=ALU.mult),
                                                [r_otk[b], r_dn], [r_on]))
                        for gi, (lo, hi) in enumerate(((0, 256), (256, 640), (640, 1024))):
                            ops.append(lambda gi=gi, lo=lo, hi=hi: S.op("act", lambda E: E.activation(
                                out=tm[:, lo:hi], in_=on_[:, lo:hi], func=AF.Square, accum_out=dn[:, 32 + gi:33 + gi]),
                                                                        [r_on], [r_tm, r_dn]))
                        ops.append(lambda: S.op("dve", lambda E: E.tensor_scalar(out=dn[:, 36:37], in0=dn[:, 32:33], scalar1=1.0 / 256, scalar2=EPS,
                                                                                 op0=ALU.mult, op1=ALU.add), [r_dn], [r_dn]))
                        ops.append(lambda: S.op("dve", lambda E: E.tensor_scalar(out=dn[:, 37:39], in0=dn[:, 33:35], scalar1=1.0 / 384, scalar2=EPS,
                                                                                 op0=ALU.mult, op1=ALU.add), [r_dn], [r_dn]))
                        ops.append(lambda: S.op("act", lambda E: E.activation(out=dn[:, 36:39], in_=dn[:, 36:39], func=AF.Sqrt), [r_dn], [r_dn]))
                        ops.append(lambda: S.op("dve", lambda E: E.reciprocal(out=dn[:, 40:43], in_=dn[:, 36:39]), [r_dn], [r_dn]))
                        for gi, (lo, hi) in enumerate(((0, 256), (256, 640), (640, 1024))):
                            ops.append(lambda gi=gi, lo=lo, hi=hi: S.op("dve", lambda E: E.scalar_tensor_tensor(
                                out=yb_[:, lo:hi], in0=on_[:, lo:hi], scalar=dn[:, 40 + gi:41 + gi], in1=ggr[:, lo:hi],
                                op0=ALU.mult, op1=ALU.mult), [r_on, r_dn, r_bc], [r_yb_]))

                        def f_tr():
                            for kc in range(8):
                                S.op("pe", lambda E, kc=kc: E.transpose(pS[k][:, kc * 128:(kc + 1) * 128],
                                                                        yb_[:, kc * 128:(kc + 1) * 128], idf[:]),
                                     [r_yb_, r_id], [r_pS[k]])
                        ops.append(f_tr)
                        ops.append(lambda: S.op("act", lambda E: E.activation(
                            out=yT_[:, 0:4, :], in_=pS[k][:, 0:512].rearrange("p (k t) -> p k t", k=4), func=AF.Copy), [r_pS[k]], [r_yT_]))
                        ops.append(lambda: S.op("dve", lambda E: E.tensor_copy(
                            out=yT_[:, 4:8, :], in_=pS[k][:, 512:1024].rearrange("p (k t) -> p k t", k=4)), [r_pS[k]], [r_yT_]))

                        def f_mm():
                            for hf in range(2):
                                for kc in range(8):
                                    S.op("pe", lambda E, kc=kc, hf=hf: E.matmul(pS[k][:, hf * 512:(hf + 1) * 512], lhsT=yT_[:, kc, :],
                                                                               rhs=wo[:, kc, hf * 512:(hf + 1) * 512],
                                                                               start=(kc == 0), stop=(kc == 7)),
                                         [r_yT_, r_wo], [r_pS[k]])
                        ops.append(f_mm)
                        ops.append(lambda: S.op("dve", lambda E: E.tensor_tensor(out=tm[:], in0=pS[k][:], in1=gta[:], op=ALU.mult),
                                                [r_pS[k], r_bc], [r_tm]))
                        ops.append(lambda: S.op("pool", lambda E: E.tensor_tensor(out=xr[xs][:], in0=xr[xs][:], in1=tm[:], op=ALU.add),
                                                [r_xr[xs], r_tm], [r_xr[xs]]))
                        ops.append(lambda: S.dma("sp", x1[n * 128:(n + 1) * 128, :], xr[xs][:], [r_xr[xs]], [R["x1"]]))
                        return ops

                    NFRONT = 15

                    def run_pair(o0, o1):
                        for f0, f1 in zip(o0, o1):
                            f0()
                            f1()

                    c0, c1 = comb_ops(0), comb_ops(1)
                    run_pair(c0[:NFRONT], c1[:NFRONT])
                    if G + 1 < NG:
                        emit(nasw_units(G + 1, [0, 1]))
                    run_pair(c0[NFRONT:], c1[NFRONT:])
                    c2, c3 = comb_ops(2), comb_ops(3)
                    run_pair(c2[:NFRONT], c3[:NFRONT])
                    if G + 1 < NG:
                        emit(nasw_units(G + 1, [2, 3]))
                    run_pair(c2[NFRONT:], c3[NFRONT:])
                S.barrier()

        def phase_c(l, x_dst, r_xdst, last):
            with ExitStack() as st:
                sb = lambda n, s, d: st.enter_context(nc.sbuf_tensor(U(n), s, d))
                ps = lambda n, s, d: st.enter_context(nc.psum_tensor(U(n), s, d))
                wgu = sb("wgu", [128, 8, 2 * F], BF16); r_wgu = S.res("wgu", dma=True)
                wdn = sb("wdn", [128, NFC, D], BF16); r_wdn = S.res("wdn", dma=True)
                idb = sb("idb", [128, 128], BF16); r_idb = S.res("idbC", dma=True)
                gtf = sb("gtf", [128, D], F32); gfin = sb("gfin", [128, D], F32); r_bc = S.res("bcC", dma=True)
                a1, a0, r_a = load_mod_cols(st, l, 4 * D, 3 * D, g_ffn_t, "C")
                xt = [sb("xtc%d" % i, [128, D], F32) for i in range(4)]; r_xt = [S.res("xtc%d" % i, dma=True) for i in range(4)]
                ssq = sb("ssqc", [128, 16], F32); r_ssq = S.res("ssqc")
                fsq = sb("fsq", [128, 4], F32); r_fsq = S.res("fsq")
                xo = sb("xoc", [128, D], F32); r_xo = S.res("xoc", dma=True)
                xn = sb("xnc", [128, 4, D], BF16); r_xn = [S.res("xnc%d" % b) for b in range(4)]
                hT = sb("hTc", [128, 8, 512], BF16); r_hT = S.res("hTc")
                actT = sb("actT", [128, NFC, 512], BF16); r_actT = [S.res("actT%d" % j) for j in range(NFC)]
                sg = [sb("sg%d" % i, [128, 512], F32) for i in range(2)]; r_sg = [S.res("sg%d" % i) for i in range(2)]
                tmp = sb("tmpC", [128, D], F32); r_tmp = S.res("tmpC")
                psT = [ps("psTc%d" % i, [128, 1024], BF16) for i in range(2)]; r_psT = [S.res("psTc%d" % i) for i in range(2)]
                pg = [ps("pg%d" % i, [128, 512], F32) for i in range(2)]; r_pg = [S.res("pg%d" % i) for i in range(2)]
                pu = [ps("pu%d" % i, [128, 512], F32) for i in range(2)]; r_pu = [S.res("pu%d" % i) for i in range(2)]
                pd = ps("pd", [128, 1024], F32); r_pd = S.res("pd")

                S.dma("sp", idb[:], idb_in, [R_in], [r_idb])
                S.dma("sp", gtf[:], modrow[l:l + 1, 5 * D:6 * D].partition_broadcast(128), [R["modrow"]], [r_bc])
                S.dma("sp", gfin[:], g_final[0:1, :].partition_broadcast(128), [R_in], [r_bc])
                for half in range(2):
                    for hf in range(2):
                        for kc in range(8):
                            c0 = hf * F + half * 1408
                            S.dma("pool", wgu[:, kc, c0:c0 + 1408], w_gu[l, kc * 128:(kc + 1) * 128, c0:c0 + 1408], [R_in], [r_wgu])
                for j in range(NFC):
                    S.dma("pool", wdn[:, j, :], w_down[l, j * 128:(j + 1) * 128, :], [R_in], [r_wdn])

                def load_x(G):
                    for b in range(4):
                        S.dma("sp", xt[b][:], x1[(G * 4 + b) * 128:(G * 4 + b + 1) * 128, :], [R["x1"]], [r_xt[b]])

                def nops():
                    return norm_ops([xt[b][:] for b in range(4)], r_xt, 4, ssq, r_ssq, None, None, xn, r_xn)

                def do_tr():
                    tr_part(4, xn, r_xn, psT, r_psT, hT, r_hT, a1, a0, r_a, idb, r_idb)

                load_x(0)
                for f in nops():
                    f()
                do_tr()
                for G in range(NG):
                    pend = []
                    if G + 1 < NG:
                        load_x(G + 1)
                        pend = nops()
                    for j in range(NFC):
                        p = j % 2
                        for kc in range(8):
                            S.op("pe", lambda E, j=j, kc=kc, p=p: E.matmul(pg[p][:], lhsT=wgu[:, kc, j * 128:(j + 1) * 128], rhs=hT[:, kc, :],
                                                                          start=(kc == 0), stop=(kc == 7)), [r_wgu, r_hT], [r_pg[p]])
                        for kc in range(8):
                            S.op("pe", lambda E, j=j, kc=kc, p=p: E.matmul(pu[p][:], lhsT=wgu[:, kc, F + j * 128:F + (j + 1) * 128], rhs=hT[:, kc, :],
                                                                          start=(kc == 0), stop=(kc == 7)), [r_wgu, r_hT], [r_pu[p]])
                        S.op("act", lambda E, p=p: E.activation(out=sg[p][:], in_=pg[p][:], func=AF.Silu), [r_pg[p]], [r_sg[p]])
                        S.op("dve", lambda E, j=j, p=p: E.tensor_tensor(out=actT[:, j, :], in0=pu[p][:], in1=sg[p][:], op=ALU.mult),
                             [r_pu[p], r_sg[p]], [r_actT[j]])
                        if j >= 3 and pend:
                            pend.pop(0)()
                    while pend:
                        pend.pop(0)()
                    if G + 1 < NG:
                        do_tr()
                    for b in range(4):
                        n = G * 4 + b
                        S.dma("sp", xo[:], x1[n * 128:(n + 1) * 128, :], [R["x1"]], [r_xo])
                        for hf in range(2):
                            for j in range(NFC):
                                S.op("pe", lambda E, j=j, b=b, hf=hf: E.matmul(pd[:, hf * 512:(hf + 1) * 512], lhsT=actT[:, j, b * 128:(b + 1) * 128],
                                                                              rhs=wdn[:, j, hf * 512:(hf + 1) * 512],
                                                                              start=(j == 0), stop=(j == NFC - 1)),
                                     [r_actT[j], r_wdn], [r_pd])
                        S.op("dve", lambda E: E.tensor_tensor(out=tmp[:], in0=pd[:], in1=gtf[:], op=ALU.mult), [r_pd, r_bc], [r_tmp])
                        S.op("pool", lambda E: E.tensor_tensor(out=xo[:], in0=xo[:], in1=tmp[:], op=ALU.add), [r_xo, r_tmp], [r_xo])
                        if last:
                            S.op("act", lambda E: E.activation(out=tmp[:], in_=xo[:], func=AF.Square, accum_out=fsq[:, 0:1]),
                                 [r_xo], [r_tmp, r_fsq])
                            S.op("dve", lambda E: E.tensor_scalar(out=fsq[:, 1:2], in0=fsq[:, 0:1], scalar1=1.0 / D, scalar2=EPS,
                                                                  op0=ALU.mult, op1=ALU.add), [r_fsq], [r_fsq])
                            S.op("act", lambda E: E.activation(out=fsq[:, 1:2], in_=fsq[:, 1:2], func=AF.Sqrt), [r_fsq], [r_fsq])
                            S.op("dve", lambda E: E.reciprocal(out=fsq[:, 2:3], in_=fsq[:, 1:2]), [r_fsq], [r_fsq])
                            S.op("dve", lambda E: E.scalar_tensor_tensor(out=xo[:], in0=xo[:], scalar=fsq[:, 2:3], in1=gfin[:],
                                                                         op0=ALU.mult, op1=ALU.mult), [r_xo, r_fsq, r_bc], [r_xo])
                        S.dma("sp", x_dst[n * 128:(n + 1) * 128, :], xo[:], [r_xo], [r_xdst])
                S.barrier()

        def norm_transpose_c(xts, r_xts, ssq, r_ssq, junk, r_junk, xn, r_xn, psT, r_psT, hT, r_hT, a1, a0, r_a, idb, r_idb):
            norm_transpose(xts, r_xts, 4, ssq, r_ssq, junk, r_junk, xn, r_xn, psT, r_psT, hT, r_hT, a1, a0, r_a, idb, r_idb)

        phase_mod()
        x_src, r_xsrc = x_in, R_in
        done = False
        for l in range(L):
            if done:
                break
            phase_a(l, x_src, r_xsrc)
            if stop_after == ("a", l):
                done = True
                break
            exchange()
            if stop_after == ("x", l):
                done = True
                break
            phase_b(l, x_src, r_xsrc)
            if stop_after == ("b", l):
                done = True
                break
            last = (l == L - 1)
            if last:
                phase_c(l, out, R["out"], True)
            else:
                phase_c(l, x2, R["x2"], False)
            if stop_after == ("c", l):
                done = True
                break
            x_src, r_xsrc = x2, R["x2"]
        if dbg:
            srcs = {"q_na": q_na, "q_sw": q_sw, "q_ax": q_ax, "ktns": ktns, "vns": vns, "ktax_all": ktax_all.ap(),
                    "vax_all": vax_all[0].ap(), "hk_all": hk_all.ap(), "hv_all": hv_all.ap(), "x1": x1, "x2": x2, "modrow": modrow,
                    "ktax_loc": ktax_loc.ap(), "vax_loc": vax_loc[0].ap()}
            for dn in dbg_outs:
                S.dma("sp", dbg_outs[dn], srcs[dn], [R["qscr"], R["kvscr"], R["x1"], R["x2"], R["modrow"], R_gath], [R["dbg"]])
        fin = {}
        for k in (R["out"], R["dbg"]):
            if k.w is not None:
                fin[k.w[0]] = k.w[1]
        S._wait("sp", fin)
    return nc


def make_in_maps(x, c, w_mod, b_mod, g_attn, w_in, rpb_na, sink_sw, t5_table, gq_ax, gk_ax,
                 g_group, w_o, g_ffn, w_gu, w_down, g_final):
    f32 = lambda a: np.ascontiguousarray(np.asarray(a, dtype=np.float32))
    x = f32(x); c = f32(c)
    perm = _win_perm()
    w_in_p = np.ascontiguousarray(f32(w_in)[:, :, perm])
    bidx = _t5_bucket_idx()
    gqk = np.concatenate([np.tile(f32(gq_ax), (1, 6)), np.tile(f32(gk_ax), (1, 2))], axis=1)
    tcol = lambda g: np.ascontiguousarray(f32(g).reshape(L, 8, 128).transpose(0, 2, 1))
    shared = {
        "w_mod": f32(w_mod), "b_mod": f32(b_mod), "g_attn_t": tcol(g_attn), "g_ffn_t": tcol(g_ffn),
        "w_in": w_in_p, "w_o": f32(w_o), "w_gu": f32(w_gu), "w_down": f32(w_down), "g_group": f32(g_group),
        "gqk": np.ascontiguousarray(gqk), "g_final": f32(g_final).reshape(1, D), "sink": f32(sink_sw),
        "idb_in": np.eye(128, dtype=np.float32).astype(ml_dtypes.bfloat16), "idf_in": np.eye(128, dtype=np.float32),
    }
    rpb = f32(rpb_na); t5 = f32(t5_table)
    per_q = {}
    for q in range(4):
        rc, rs = _rope_tables(q)
        na = np.stack([_na_tables(rpb[l], q) for l in range(L)], 0).reshape(L, 5, 128, 4 * 7 * 128)
        per_q[q] = {"rope_c": rc, "rope_s": rs, "na_tab": np.ascontiguousarray(na),
                    "sw_tab": np.ascontiguousarray(_sw_tables(t5, q, bidx).reshape(5, 128, 768)), "idx_in": _idx_table(q)}
    in_maps = []
    for i in range(8):
        b, q = i // 4, i % 4
        m = dict(shared)
        m.update(per_q[q])
        m["x_in"] = np.ascontiguousarray(x[b, q * T:(q + 1) * T, :])
        m["c_t"] = np.ascontiguousarray(c[b].reshape(8, 128).T)
        in_maps.append(m)
    return in_maps


_NC_CACHE = {}


def kernel(**inputs):
    in_maps = make_in_maps(**inputs)
    if "nc" not in _NC_CACHE:
        _NC_CACHE["nc"] = build()
    nc = _NC_CACHE["nc"]
    res = run_bass_kernel_spmd(nc, in_maps, core_ids=list(range(8)))
    outp = np.empty((2, 4 * T, D), np.float32)
    for i in range(8):
        b, q = i // 4, i % 4
        outp[b, q * T:(q + 1) * T, :] = res.results[i]["out"]
    return outp
```

```python
import math
import numpy as np
import ml_dtypes
from contextlib import ExitStack
import concourse.bass as bass
import concourse.mybir as mybir
from concourse.bass_utils import run_bass_kernel_spmd

F32 = mybir.dt.float32
BF16 = mybir.dt.bfloat16
I32 = mybir.dt.int32
AF = mybir.ActivationFunctionType
ALU = mybir.AluOpType
AXL = mybir.AxisListType

D = 1024
T = 4096
NB = 32
NG = 8
F = 2816
NFC = 22
L = 2
EPS = 1e-6
MASKV = -30000.0


class Res:
    __slots__ = ("name", "w", "r", "dsem")

    def __init__(self, name, dsem=None):
        self.name = name
        self.w = None
        self.r = {}
        self.dsem = dsem


class Sched:
    def __init__(self, nc, stack):
        self.nc = nc
        self.stack = stack
        self.eng = {"pe": nc.tensor, "act": nc.scalar, "dve": nc.vector,
                    "pool": nc.gpsimd, "sp": nc.sync}
        self.sems = {}
        self.cnt = {}
        self.seen = {e: {} for e in self.eng}
        for e in self.eng:
            self.sems[e] = stack.enter_context(nc.semaphore("s_" + e))
            self.cnt[e] = 0
        self.nd = 0

    def res(self, name, dma=False):
        r = Res(name)
        if dma:
            key = "d%d_%s" % (self.nd, name)
            self.nd += 1
            self.sems[key] = self.stack.enter_context(self.nc.semaphore(key))
            self.cnt[key] = 0
            r.dsem = key
        return r

    def _wait(self, e, deps):
        E = self.eng[e]
        seen = self.seen[e]
        for k, v in deps.items():
            if e == "pe" and k == "pe":
                continue
            if seen.get(k, 0) < v:
                if k not in self.eng:
                    v = self.cnt[k]
                E.wait_ge(self.sems[k], v)
                seen[k] = v

    def op(self, e, fn, reads=(), writes=()):
        deps = {}
        for r in reads:
            if r.w is not None:
                k, v = r.w
                if deps.get(k, 0) < v:
                    deps[k] = v
        for w in writes:
            if w.w is not None:
                k, v = w.w
                if deps.get(k, 0) < v:
                    deps[k] = v
            for k, v in w.r.items():
                if deps.get(k, 0) < v:
                    deps[k] = v
        self._wait(e, deps)
        ins = fn(self.eng[e])
        ins.then_inc(self.sems[e], 1)
        self.cnt[e] += 1
        n = self.cnt[e]
        for r in reads:
            if r.r.get(e, 0) < n:
                r.r[e] = n
        for w in writes:
            w.w = (e, n)
            w.r = {}
        return ins

    def dma(self, e, out, in_, reads, writes, indirect=None, **kw):
        dst = writes[0]
        assert dst.dsem is not None, dst.name
        deps = {}
        for r in reads:
            if r.w is not None:
                k, v = r.w
                if deps.get(k, 0) < v:
                    deps[k] = v
        for w in writes:
            if w.w is not None and w.w[0] != dst.dsem:
                k, v = w.w
                if deps.get(k, 0) < v:
                    deps[k] = v
            for k, v in w.r.items():
                if deps.get(k, 0) < v:
                    deps[k] = v
        self._wait(e, deps)
        if indirect is not None:
            ins = self.eng[e].indirect_dma_start(out=out, out_offset=None, in_=in_, in_offset=indirect)
        else:
            ins = self.eng[e].dma_start(out=out, in_=in_, **kw)
        ins.then_inc(self.sems[dst.dsem], 16)
        self.cnt[dst.dsem] += 16
        n = self.cnt[dst.dsem]
        for r in reads:
            if r.r.get(dst.dsem, 0) < n:
                r.r[dst.dsem] = n
        for w in writes:
            w.w = (dst.dsem, n)
            w.r = {}
        return ins

    def barrier(self):
        tot = {k: v for k, v in self.cnt.items() if v > 0}
        for e in self.eng:
            self._wait(e, dict(tot))


QA, KA, VA, QB, KB, VB, QC, KC, VC = 0, 256, 512, 768, 1152, 1280, 1408, 1792, 1920


def _win_perm():
    p = []
    p += list(range(QA, QA + 256))
    for r in range(3):
        p += list(range(QB + r * 64, QB + r * 64 + 64)) + list(range(QB + (3 + r) * 64, QB + (3 + r) * 64 + 64))
    p += list(range(KA, KA + 256))
    p += list(range(KB, KB + 128))
    for r in range(3):
        p += list(range(QC + r * 64, QC + r * 64 + 64)) + list(range(QC + (3 + r) * 64, QC + (3 + r) * 64 + 64))
    p += list(range(KC, KC + 128))
    p += list(range(VA, VA + 256)) + list(range(VB, VB + 128)) + list(range(VC, VC + 128))
    assert len(p) == 2048 and len(set(p)) == 2048
    return np.array(p)


def _t5_bucket_idx():
    import jax
    import jax.numpy as jnp
    cpu = jax.devices("cpu")[0]
    with jax.default_device(cpu):
        rel = jnp.arange(-383, 384)
        nb = 16
        ret = (rel > 0).astype(jnp.int32) * nb
        n = jnp.abs(rel)
        max_exact = nb // 2
        nf = jnp.maximum(n, max_exact).astype(jnp.float32)
        large = max_exact + (jnp.log(nf / max_exact) / math.log(128 / max_exact) * (nb - max_exact)).astype(jnp.int32)
        large = jnp.minimum(large, nb - 1)
        out = ret + jnp.where(n < max_exact, n, large)
        return np.asarray(out)


def _na_tables(rpb, q):
    out = np.empty((5, 128, 4, 7, 128), np.float32)
    k = np.arange(128)[:, None, None]
    kb = np.arange(7)[None, :, None]
    qq = np.arange(128)[None, None, :]
    for v, j in enumerate((0, 1, 2, 30, 31)):
        n = q * 32 + j
        if v == 2:
            n = 64
        m = n - 3 + kb
        kr = 2 * m + k // 64
        kc = (k % 64) + 0 * kb + 0 * qq
        qr = 2 * n + qq // 64
        qc = qq % 64
        rs = np.clip(qr - 4, 0, 256 - 8)
        cs = np.clip(qc - 8, 0, 64 - 16)
        valid = (m >= 0) & (m < 128) & (kr >= rs) & (kr < rs + 8) & (kc >= cs) & (kc < cs + 16)
        dr = np.clip(kr - qr + 7, 0, 14) + 0 * kc
        dc = np.clip(kc - qc + 15, 0, 30) + 0 * kr
        valid = np.broadcast_to(valid, dr.shape)
        for h in range(4):
            out[v, :, h] = np.where(valid, rpb[h][dr, dc], np.float32(MASKV))
    return out


def _sw_tables(t5, q, bidx):
    out = np.empty((5, 128, 2, 3, 128), np.float32)
    k = np.arange(128)[:, None]
    qq = np.arange(128)[None, :]
    for t in range(3):
        rel = (t - 1) * 128 + k - qq
        valid = np.abs(rel) <= 128
        b = bidx[rel + 383]
        for g in range(2):
            for r in range(3):
                out[t, :, g, r, :] = np.where(valid, t5[b, g * 3 + r], np.float32(MASKV))
    out[3] = out[0] if q != 0 else np.float32(MASKV)
    out[4] = out[2] if q != 3 else np.float32(MASKV)
    return out.reshape(5, 128, 2, 384)


def _rope_tables(q):
    t = np.arange(T) + q * T
    row = (t // 64).astype(np.float32)
    col = (t % 64).astype(np.float32)
    freqs = (np.float32(10000.0) ** (-np.arange(0, 32, 2, dtype=np.float32) / np.float32(32))).astype(np.float32)
    ang = np.stack([row[:, None] * freqs, col[:, None] * freqs], axis=1).astype(np.float32)
    return np.cos(ang).astype(np.float32).reshape(T, 32), np.sin(ang).astype(np.float32).reshape(T, 32)


def _idx_table(q):
    p = np.arange(128)
    idx = np.zeros((128, 12), np.int32)
    lq, rq = (q - 1) % 4, (q + 1) % 4
    for c in range(3):
        idx[:, c] = lq * 768 + 384 + c * 128 + p
        idx[:, 3 + c] = rq * 768 + 0 + c * 128 + p
        idx[:, 6 + c] = lq * 768 + 384 + c * 128 + p
        idx[:, 9 + c] = rq * 768 + 0 + c * 128 + p
    return idx


def build(stop_after=None, dbg=None):
    nc = bass.Bass("TRN2", target_bir_lowering=False)
    _uid = [0]

    def U(n):
        _uid[0] += 1
        return "%s_%d" % (n, _uid[0])

    def din(name, shape, dt=F32):
        return nc.dram_tensor(name, list(shape), dt, kind="ExternalInput").ap()

    x_in = din("x_in", [T, D])
    c_t = din("c_t", [128, 8])
    w_mod = din("w_mod", [L, D, 6 * D])
    b_mod = din("b_mod", [L, 6 * D])
    g_attn_t = din("g_attn_t", [L, 128, 8])
    g_ffn_t = din("g_ffn_t", [L, 128, 8])
    w_in = din("w_in", [L, D, 2048])
    w_o = din("w_o", [L, D, D])
    w_gu = din("w_gu", [L, D, 2 * F])
    w_down = din("w_down", [L, F, D])
    g_group = din("g_group", [L, D])
    gqk = din("gqk", [L, 512])
    g_final = din("g_final", [1, D])
    sink = din("sink", [L, 6])
    rope_c = din("rope_c", [T, 32])
    rope_s = din("rope_s", [T, 32])
    na_tab = din("na_tab", [L, 5, 128, 4 * 7 * 128])
    sw_tab = din("sw_tab", [5, 128, 2 * 384])
    idx_in = din("idx_in", [128, 12], I32)
    idb_in = din("idb_in", [128, 128], BF16)
    idf_in = din("idf_in", [128, 128], F32)
    out = nc.dram_tensor("out", [T, D], F32, kind="ExternalOutput").ap()
    dbg_outs = {}
    for (dn, dshape, ddt) in (dbg or []):
        dbg_outs[dn] = nc.dram_tensor("dbg_" + dn, list(dshape), ddt, kind="ExternalOutput").ap()

    def dscr(name, shape, dt):
        return nc.dram_tensor(name, list(shape), dt)

    modrow = dscr("modrow", [L, 6 * D], F32).ap()
    x1 = dscr("x1", [T, D], F32).ap()
    x2 = dscr("x2", [T, D], F32).ap()
    q_na = dscr("q_na", [2, 128, T], BF16).ap()
    q_sw = dscr("q_sw", [128, NB, 3, 128], BF16).ap()
    q_ax = dscr("q_ax", [3, 128, T], BF16).ap()
    ktns = dscr("ktns", [3, 128, T], BF16).ap()
    vns = dscr("vns", [T, 390], BF16).ap()
    ktax_loc = dscr("ktax_loc", [128, T], BF16)
    ktax_all = dscr("ktax_all", [4 * 128, T], BF16)
    vax_loc = [dscr("vax_loc%d" % i, [T // 2, 130], BF16) for i in range(2)]
    vax_all = [dscr("vax_all%d" % i, [4 * T // 2, 130], BF16) for i in range(2)]
    hk_loc = dscr("hk_loc", [768, 384], BF16)
    hk_all = dscr("hk_all", [4 * 768, 384], BF16)
    hv_loc = dscr("hv_loc", [768, 390], BF16)
    hv_all = dscr("hv_all", [4 * 768, 390], BF16)

    with ExitStack() as top:
        S = Sched(nc, top)
        R = {n: S.res(n, dma=True) for n in
             ("modrow", "x1", "x2", "qscr", "kvscr", "halo", "out", "dbg")}
        R_in = Res("inputs")
        R_gath = S.res("gathered")

        def phase_mod():
            with ExitStack() as st:
                sb = lambda n, s, d: st.enter_context(nc.sbuf_tensor(U(n), s, d))
                ct = sb("ct", [128, 8], F32); r_ct = S.res("ct", dma=True)
                bm = sb("bm", [1, L * 6 * D], F32); r_bm = S.res("bm", dma=True)
                ms = sb("ms", [1, L * 6 * D], F32); r_ms = S.res("ms")
                wm = [sb("wm%d" % i, [128, 8, 512], F32) for i in range(2)]
                r_wm = [S.res("wm%d" % i, dma=True) for i in range(2)]
                pm = [st.enter_context(nc.psum_tensor(U("pm%d" % i), [128, 512], F32)) for i in range(2)]
                r_pm = [S.res("pm%d" % i) for i in range(2)]
                S.dma("sp", ct[:], c_t, [R_in], [r_ct])
                S.dma("sp", bm[:], b_mod.rearrange("l n -> (l n)").rearrange("(o n) -> o n", o=1), [R_in], [r_bm])
                S.op("act", lambda E: E.activation(out=ct[:], in_=ct[:], func=AF.Silu), [r_ct], [r_ct])
                it = 0
                for l in range(L):
                    for j in range(12):
                        s = it % 2
                        S.dma("sp", wm[s][:], w_mod[l, :, j * 512:(j + 1) * 512].rearrange("(kc p) n -> p kc n", p=128),
                              [R_in], [r_wm[s]])
                        for kc in range(8):
                            S.op("pe", lambda E, kc=kc, s=s: E.matmul(pm[s][0:1, :], lhsT=ct[:, kc:kc + 1], rhs=wm[s][:, kc, :],
                                                                     start=(kc == 0), stop=(kc == 7)),
                                 [r_ct, r_wm[s]], [r_pm[s]])
                        o = l * 6 * D + j * 512
                        S.op("dve", lambda E, s=s, o=o: E.tensor_tensor(out=ms[0:1, o:o + 512], in0=pm[s][0:1, :],
                                                                       in1=bm[0:1, o:o + 512], op=ALU.add),
                             [r_pm[s], r_bm], [r_ms])
                        it += 1
                S.dma("sp", modrow.rearrange("l n -> (l n)").rearrange("(o n) -> o n", o=1), ms[:], [r_ms], [R["modrow"]])
                S.barrier()

        def norm_ops(xt, r_xt, nblk, ssq, r_ssq, junk, r_junk, xn, r_xn):
            ops = []
            for b in range(nblk):
                if junk is None:
                    ops.append(lambda b=b: S.op("act", lambda E: E.activation(out=xn[:, b, :], in_=xt[b], func=AF.Square,
                                                                              accum_out=ssq[:, b:b + 1]),
                                                [r_xt[b]], [r_xn[b], r_ssq]))
                else:
                    ops.append(lambda b=b: S.op("act", lambda E: E.activation(out=junk[:], in_=xt[b], func=AF.Square,
                                                                              accum_out=ssq[:, b:b + 1]),
                                                [r_xt[b]], [r_junk, r_ssq]))
            ops.append(lambda: S.op("dve", lambda E: E.tensor_scalar(out=ssq[:, 4:4 + nblk], in0=ssq[:, 0:nblk], scalar1=1.0 / D, scalar2=EPS,
                                                                     op0=ALU.mult, op1=ALU.add), [r_ssq], [r_ssq]))
            ops.append(lambda: S.op("act", lambda E: E.activation(out=ssq[:, 4:4 + nblk], in_=ssq[:, 4:4 + nblk], func=AF.Sqrt), [r_ssq], [r_ssq]))
            ops.append(lambda: S.op("dve", lambda E: E.reciprocal(out=ssq[:, 8:8 + nblk], in_=ssq[:, 4:4 + nblk]), [r_ssq], [r_ssq]))
            for b in range(nblk):
                if b % 2 == 0:
                    ops.append(lambda b=b: S.op("dve", lambda E: E.tensor_scalar(out=xn[:, b, :], in0=xt[b], scalar1=ssq[:, 8 + b:9 + b],
                                                                                 scalar2=None, op0=ALU.mult), [r_xt[b], r_ssq], [r_xn[b]]))
                else:
                    ops.append(lambda b=b: S.op("act", lambda E: E.activation(out=xn[:, b, :], in_=xt[b], func=AF.Identity,
                                                                              scale=ssq[:, 8 + b:9 + b]), [r_xt[b], r_ssq], [r_xn[b]]))
            return ops

        def norm_part(xt, r_xt, nblk, ssq, r_ssq, junk, r_junk, xn, r_xn):
            for f in norm_ops(xt, r_xt, nblk, ssq, r_ssq, junk, r_junk, xn, r_xn):
                f()

        def tr_part(nblk, xn, r_xn, psT, r_psT, hT, r_hT, a1, a0, r_a, idb, r_idb):
            for kc in range(8):
                s = kc % 2
                for b in range(nblk):
                    S.op("pe", lambda E, b=b, kc=kc, s=s: E.transpose(psT[s][:, b * 128:(b + 1) * 128],
                                                                     xn[:, b, kc * 128:(kc + 1) * 128], idb[:]),
                         [r_xn[b], r_idb], [r_psT[s]])
                if kc % 2 == 0:
                    S.op("act", lambda E, kc=kc, s=s: E.activation(out=hT[:, kc, 0:nblk * 128], in_=psT[s][:, 0:nblk * 128],
                                                                   func=AF.Identity, scale=a1[:, kc:kc + 1], bias=a0[:, kc:kc + 1]),
                         [r_psT[s], r_a], [r_hT])
                else:
                    S.op("dve", lambda E, kc=kc, s=s: E.tensor_scalar(out=hT[:, kc, 0:nblk * 128], in0=psT[s][:, 0:nblk * 128],
                                                                      scalar1=a1[:, kc:kc + 1], scalar2=a0[:, kc:kc + 1],
                                                                      op0=ALU.mult, op1=ALU.add),
                         [r_psT[s], r_a], [r_hT])

        def norm_transpose(xt, r_xt, nblk, ssq, r_ssq, junk, r_junk, xn, r_xn, psT, r_psT, hT, r_hT, a1, a0, r_a, idb, r_idb):
            norm_part(xt, r_xt, nblk, ssq, r_ssq, junk, r_junk, xn, r_xn)
            tr_part(nblk, xn, r_xn, psT, r_psT, hT, r_hT, a1, a0, r_a, idb, r_idb)

        def load_mod_cols(st, l, off_sc, off_sh, g_t, name):
            sb = lambda n, s, d: st.enter_context(nc.sbuf_tensor(U(n), s, d))
            a1 = sb(name + "a1", [128, 8], F32)
            a0 = sb(name + "a0", [128, 8], F32)
            gt = sb(name + "gt", [128, 8], F32)
            r_a = S.res(name + "a", dma=True)
            S.dma("sp", a1[:], modrow[l, off_sc:off_sc + D].rearrange("(c p) -> p c", p=128), [R["modrow"]], [r_a],
                  allow_slow_non_contiguous=True)
            S.dma("sp", a0[:], modrow[l, off_sh:off_sh + D].rearrange("(c p) -> p c", p=128), [R["modrow"]], [r_a],
                  allow_slow_non_contiguous=True)
            S.dma("sp", gt[:], g_t[l], [R_in], [r_a])
            S.op("dve", lambda E: E.tensor_scalar(out=a1[:], in0=a1[:], scalar1=1.0, scalar2=None, op0=ALU.add), [r_a], [r_a])
            S.op("dve", lambda E: E.tensor_tensor(out=a1[:], in0=a1[:], in1=gt[:], op=ALU.mult), [r_a], [r_a])
            return a1, a0, r_a

        def phase_a(l, x_src, r_xsrc):
            with ExitStack() as st:
                sb = lambda n, s, d: st.enter_context(nc.sbuf_tensor(U(n), s, d))
                ps = lambda n, s, d: st.enter_context(nc.psum_tensor(U(n), s, d))
                win = sb("win", [128, 8, 2048], BF16); r_win = S.res("win", dma=True)
                idb = sb("idb", [128, 128], BF16); r_idb = S.res("idb", dma=True)
                gq = sb("gq", [128, 512], F32); r_gq = S.res("gq", dma=True)
                a1, a0, r_a = load_mod_cols(st, l, D, 0, g_attn_t, "A")
                xt = [sb("xt%d" % i, [128, 4, D], F32) for i in range(2)]
                r_xt = [[S.res("xt%d_%d" % (i, b), dma=True) for b in range(4)] for i in range(2)]
                rc = [sb("rc%d" % i, [128, 4, 32], F32) for i in range(2)]
                rs = [sb("rs%d" % i, [128, 4, 32], F32) for i in range(2)]
                r_rope = [S.res("rope%d" % i, dma=True) for i in range(2)]
                ssq2 = [sb("ssq%d" % i, [128, 12], F32) for i in range(2)]; r_ssq2 = [S.res("ssq%d" % i) for i in range(2)]
                junk = sb("junk", [128, D], BF16); r_junk = S.res("junk")
                xn2 = [sb("xn%d" % i, [128, 4, D], BF16) for i in range(2)]
                r_xn2 = [[S.res("xn%d_%d" % (i, b)) for b in range(4)] for i in range(2)]
                hT2 = [sb("hT%d" % i, [128, 8, 512], BF16) for i in range(2)]; r_hT2 = [S.res("hT%d" % i) for i in range(2)]
                fm = [sb("fm%d" % i, [128, 8, 512], BF16) for i in range(2)]; r_fm = [S.res("fm%d" % i) for i in range(2)]
                axs = [sb("axs%d" % i, [128, 4, 512], BF16) for i in range(2)]; r_axs = [S.res("axs%d" % i) for i in range(2)]
                vst = [sb("vst%d" % i, [128, 4, 8, 65], BF16) for i in range(2)]; r_vst = [S.res("vst%d" % i) for i in range(2)]
                axr2 = [sb("axr%d" % i, [128, 512], F32) for i in range(2)]; r_axr2 = [S.res("axr%d" % i) for i in range(2)]
                axq2 = [sb("axq%d" % i, [128, 512], F32) for i in range(2)]; r_axq2 = [S.res("axq%d" % i) for i in range(2)]
                axg2 = [sb("axg%d" % i, [128, 512], F32) for i in range(2)]; r_axg2 = [S.res("axg%d" % i) for i in range(2)]
                axt2 = [[sb("axt%d_%d" % (k, i), [128, 256], F32) for i in range(4)] for k in range(2)]
                r_axt2 = [[S.res("axt%d_%d" % (k, i)) for i in range(4)] for k in range(2)]
                axo2 = [sb("axo%d" % i, [128, 512], BF16) for i in range(2)]; r_axo2 = [S.res("axo%d" % i) for i in range(2)]
                s82 = [sb("s8_%d" % i, [128, 24], F32) for i in range(2)]; r_s82 = [S.res("s8_%d" % i) for i in range(2)]
                psT = [ps("psT%d" % i, [128, 1024], BF16) for i in range(2)]; r_psT = [S.res("psT%d" % i) for i in range(2)]
                pfm = [ps("pfm%d" % i, [128, 512], F32) for i in range(2)]; r_pfm = [S.res("pfm%d" % i) for i in range(2)]
                pax2 = [ps("pax%d" % i, [128, 512], F32) for i in range(2)]; r_pax2 = [S.res("pax%d" % i) for i in range(2)]
                pv = ps("pv", [128, 512], F32); r_pv = S.res("pv")
                paT = ps("paT", [128, 1024], BF16); r_paT = S.res("paT")

                for kc in range(8):
                    S.dma("pool", win[:, kc, :], w_in[l, kc * 128:(kc + 1) * 128, :], [R_in], [r_win])
                S.dma("sp", idb[:], idb_in, [R_in], [r_idb])
                S.dma("sp", gq[:], gqk[l:l + 1, :].partition_broadcast(128), [R_in], [r_gq])
                S.op("dve", lambda E: E.tensor_scalar(out=gq[:, 0:384], in0=gq[:, 0:384], scalar1=0.125, scalar2=None,
                                                      op0=ALU.mult), [r_gq], [r_gq])
                for i in range(2):
                    S.op("pool", lambda E, i=i: E.memset(vst[i][:], 1.0), [], [r_vst[i]])

                def load_group(G):
                    s = G % 2
                    for b in range(4):
                        S.dma("sp", xt[s][:, b, :], x_src[(G * 4 + b) * 128:(G * 4 + b + 1) * 128, :], [r_xsrc], [r_xt[s][b]])
                    S.dma("sp", rc[s][:], rope_c[G * 512:(G + 1) * 512, :].rearrange("(b p) c -> p b c", p=128), [R_in], [r_rope[s]])
                    S.dma("sp", rs[s][:], rope_s[G * 512:(G + 1) * 512, :].rearrange("(b p) c -> p b c", p=128), [R_in], [r_rope[s]])

                def do_norm(G):
                    k = G % 2
                    norm_part([xt[k][:, b, :] for b in range(4)], r_xt[k], 4, ssq2[k], r_ssq2[k], junk, r_junk, xn2[k], r_xn2[k])

                def do_tr(G):
                    k = G % 2
                    tr_part(4, xn2[k], r_xn2[k], psT, r_psT, hT2[k], r_hT2[k], a1, a0, r_a, idb, r_idb)

                load_group(0)
                do_norm(0)
                do_tr(0)
                for G in range(NG):
                    s = G % 2
                    hT, r_hT = hT2[s], r_hT2[s]
                    if G + 1 < NG:
                        load_group(G + 1)
                    for c in range(8):
                        p = c % 2
                        for kc in range(8):
                            S.op("pe", lambda E, c=c, kc=kc, p=p: E.matmul(pfm[p][:], lhsT=win[:, kc, c * 128:(c + 1) * 128],
                                                                          rhs=hT[:, kc, :], start=(kc == 0), stop=(kc == 7)),
                                 [r_win, r_hT], [r_pfm[p]])
                        sc = 0.125 if c < 5 else 1.0
                        if c % 2 == 0:
                            S.op("act", lambda E, c=c, p=p, sc=sc: E.activation(out=fm[s][:, c, :], in_=pfm[p][:], func=AF.Copy, scale=sc),
                                 [r_pfm[p]], [r_fm[s]])
                        else:
                            S.op("dve", lambda E, c=c, p=p, sc=sc: E.tensor_scalar(out=fm[s][:, c, :], in0=pfm[p][:], scalar1=sc,
                                                                                  scalar2=None, op0=ALU.mult),
                                 [r_pfm[p]], [r_fm[s]])
                    def MM(b):
                        pax, r_pax = pax2[b % 2], r_pax2[b % 2]
                        for kc in range(8):
                            S.op("pe", lambda E, kc=kc: E.matmul(pax[:], lhsT=hT[:, kc, b * 128:(b + 1) * 128],
                                                                rhs=win[:, kc, 1024:1536], start=(kc == 0), stop=(kc == 7)),
                                 [r_win, r_hT], [r_pax])
                        for kc in range(8):
                            S.op("pe", lambda E, kc=kc: E.matmul(pv[:], lhsT=hT[:, kc, b * 128:(b + 1) * 128],
                                                                rhs=win[:, kc, 1536:2048], start=(kc == 0), stop=(kc == 7)),
                                 [r_win, r_hT], [r_pv])
                        S.op("act", lambda E: E.activation(out=vst[s][:, b, :, 0:64],
                                                           in_=pv[:].rearrange("p (h d) -> p h d", d=64), func=AF.Copy),
                             [r_pv], [r_vst[s]])

                    def chain(b):
                        k2 = b % 2
                        pax, r_pax = pax2[k2], r_pax2[k2]
                        axr, r_axr, axq, r_axq, axg, r_axg = axr2[k2], r_axr2[k2], axq2[k2], r_axq2[k2], axg2[k2], r_axg2[k2]
                        axt, r_axt, axo, r_axo, s8, r_s8 = axt2[k2], r_axt2[k2], axo2[k2], r_axo2[k2], s82[k2], r_s82[k2]
                        xv = axg[:].rearrange("p (h a t f) -> p h a t f", h=8, a=2, t=2)
                        ov = axo[:].rearrange("p (h a t f) -> p h a t f", h=8, a=2, t=2)
                        x1v, x2v = xv[:, :, :, 0, :], xv[:, :, :, 1, :]
                        cv = rc[s][:, b, :].rearrange("p (o a f) -> p o a f", o=1, a=2).broadcast_to([128, 8, 2, 16])
                        sv = rs[s][:, b, :].rearrange("p (o a f) -> p o a f", o=1, a=2).broadcast_to([128, 8, 2, 16])
                        tv = [axt[i][:].rearrange("p (h a f) -> p h a f", h=8, a=2) for i in range(4)]
                        h3 = lambda t: t[:].rearrange("p (h d) -> p h d", d=64)
                        return [
                            lambda: S.op("act", lambda E: E.activation(out=axr[:], in_=pax[:], func=AF.Copy), [r_pax], [r_axr]),
                            lambda: S.op("pool", lambda E: E.tensor_tensor(out=axq[:], in0=axr[:], in1=axr[:], op=ALU.mult), [r_axr], [r_axq]),
                            lambda: S.op("dve", lambda E: E.tensor_reduce(out=s8[:, 0:8], in_=h3(axq), axis=AXL.X, op=ALU.add), [r_axq], [r_s8]),
                            lambda: S.op("dve", lambda E: E.tensor_scalar(out=s8[:, 8:16], in0=s8[:, 0:8], scalar1=1.0 / 64, scalar2=EPS,
                                                                          op0=ALU.mult, op1=ALU.add), [r_s8], [r_s8]),
                            lambda: S.op("act", lambda E: E.activation(out=s8[:, 8:16], in_=s8[:, 8:16], func=AF.Sqrt), [r_s8], [r_s8]),
                            lambda: S.op("dve", lambda E: E.reciprocal(out=s8[:, 16:24], in_=s8[:, 8:16]), [r_s8], [r_s8]),
                            lambda: S.op("pool", lambda E: E.tensor_tensor(
                                out=h3(axg), in0=h3(axr),
                                in1=s8[:, 16:24].rearrange("p (h o) -> p h o", o=1).broadcast_to([128, 8, 64]), op=ALU.mult),
                                         [r_axr, r_s8], [r_axg]),
                            lambda: S.op("dve", lambda E: E.tensor_tensor(out=axg[:], in0=axg[:], in1=gq[:], op=ALU.mult), [r_axg, r_gq], [r_axg]),
                            lambda: S.op("dve", lambda E: E.tensor_tensor(out=tv[0], in0=x1v, in1=cv, op=ALU.mult), [r_axg, r_rope[s]], [r_axt[0]]),
                            lambda: S.op("pool", lambda E: E.tensor_tensor(out=tv[1], in0=x2v, in1=sv, op=ALU.mult), [r_axg, r_rope[s]], [r_axt[1]]),
                            lambda: S.op("dve", lambda E: E.tensor_tensor(out=tv[2], in0=x2v, in1=cv, op=ALU.mult), [r_axg, r_rope[s]], [r_axt[2]]),
                            lambda: S.op("pool", lambda E: E.tensor_tensor(out=tv[3], in0=x1v, in1=sv, op=ALU.mult), [r_axg, r_rope[s]], [r_axt[3]]),
                            lambda: S.op("dve", lambda E: E.tensor_tensor(out=ov[:, :, :, 0, :], in0=tv[0], in1=tv[1], op=ALU.subtract),
                                         [r_axt[0], r_axt[1]], [r_axo]),
                            lambda: S.op("pool", lambda E: E.tensor_tensor(out=ov[:, :, :, 1, :], in0=tv[2], in1=tv[3], op=ALU.add),
                                         [r_axt[2], r_axt[3]], [r_axo]),
                        ]

                    def TR(b):
                        axo, r_axo = axo2[b % 2], r_axo2[b % 2]
                        for j in range(4):
                            S.op("pe", lambda E, j=j: E.transpose(paT[:, j * 128:(j + 1) * 128], axo[:, j * 128:(j + 1) * 128], idb[:]),
                                 [r_axo, r_idb], [r_paT])
                        S.op("act", lambda E: E.activation(out=axs[s][:, :, b * 128:(b + 1) * 128],
                                                           in_=paT[:, 0:512].rearrange("p (j t) -> p j t", j=4), func=AF.Copy),
                             [r_paT], [r_axs[s]])

                    def run_chains(b0, b1):
                        c0, c1 = chain(b0), chain(b1)
                        for f0, f1 in zip(c0, c1):
                            f0()
                            f1()

                    MM(0)
                    MM(1)
                    run_chains(0, 1)
                    MM(2)
                    MM(3)
                    TR(0)
                    TR(1)
                    if G + 1 < NG:
                        do_norm(G + 1)
                    run_chains(2, 3)
                    if G + 1 < NG:
                        do_tr(G + 1)
                    TR(2)
                    TR(3)
                    gsl = slice(G * 512, (G + 1) * 512)
                    for c in range(2):
                        S.dma("sp", q_na[c, :, gsl], fm[s][:, c, :], [r_fm[s]], [R["qscr"]])
                    for r in range(3):
                        S.dma("sp", q_sw[:, G * 4:(G + 1) * 4, r, :], fm[s][:, 2 + r, :].rearrange("p (b t) -> p b t", b=4),
                              [r_fm[s]], [R["qscr"]])
                    for c in range(3):
                        S.dma("sp", ktns[c, :, gsl], fm[s][:, 5 + c, :], [r_fm[s]], [R["kvscr"]])
                    for r in range(3):
                        S.dma("sp", q_ax[r, :, gsl], axs[s][:, r, :], [r_axs[s]], [R["qscr"]])
                    S.dma("sp", ktax_loc.ap()[:, gsl], axs[s][:, 3, :], [r_axs[s]], [R["kvscr"]])
                    S.dma("sp", vns[G * 512:(G + 1) * 512, :].rearrange("(b p) c -> p b c", p=128),
                          vst[s][:, :, 0:6, :].rearrange("p b h d -> p b (h d)"), [r_vst[s]], [R["kvscr"]])
                    S.dma("sp", vax_loc[G // 4].ap()[(G % 4) * 512:(G % 4 + 1) * 512, :].rearrange("(b p) c -> p b c", p=128),
                          vst[s][:, :, 6:8, :].rearrange("p b h d -> p b (h d)"), [r_vst[s]], [R["kvscr"]])
                    if G == 0 or G == NG - 1:
                        side = 0 if G == 0 else 1
                        tsl = slice(0, 384) if G == 0 else slice(128, 512)
                        bsl = slice(0, 3) if G == 0 else slice(1, 4)
                        for c in range(3):
                            S.dma("sp", hk_loc.ap()[side * 384 + c * 128: side * 384 + (c + 1) * 128, :], fm[s][:, 5 + c, tsl],
                                  [r_fm[s]], [R["halo"]])
                        S.dma("sp", hv_loc.ap()[side * 384:(side + 1) * 384, :].rearrange("(b p) c -> p b c", p=128),
                              vst[s][:, bsl, 0:6, :].rearrange("p b h d -> p b (h d)"), [r_vst[s]], [R["halo"]])
                S.barrier()

        def exchange():
            for (a, b_) in ((ktax_loc, ktax_all), (vax_loc[0], vax_all[0]), (vax_loc[1], vax_all[1]), (hk_loc, hk_all), (hv_loc, hv_all)):
                cc = nc.gpsimd.collective_compute("AllGather", ALU.bypass, replica_groups=[[0, 1, 2, 3], [4, 5, 6, 7]],
                                                  ins=[a.ap().opt()], outs=[b_.ap().opt()])
                cc.then_inc(S.sems["pool"], 1)
                S.cnt["pool"] += 1
                nc.gpsimd.wait_ge(S.sems["pool"], S.cnt["pool"])
            R_gath.w = ("pool", S.cnt["pool"])
            R_gath.r = {}
            S.barrier()

        def phase_b(l, x_src, r_xsrc):
            with ExitStack() as st:
                sb = lambda n, s, d: st.enter_context(nc.sbuf_tensor(U(n), s, d))
                ps = lambda n, s, d: st.enter_context(nc.psum_tensor(U(n), s, d))
                ktax = sb("ktax", [128, 4 * T], BF16); r_ktax = S.res("ktax", dma=True)
                vax = sb("vax", [128, 128, 130], BF16); r_vax = S.res("vax", dma=True)
                wo = sb("wo", [128, 8, D], BF16); r_wo = S.res("wo", dma=True)
                gta = sb("gta", [128, D], F32); ggr = sb("ggr", [128, D], F32); r_bc = S.res("bcB", dma=True)
                esk = sb("esk", [128, 6], F32); r_esk = S.res("esk", dma=True)
                nab = sb("nab", [128, 4 * 7 * 128], BF16); r_nab = S.res("nab", dma=True)
                swb = sb("swb", [128, 5, 768], BF16); r_swb = S.res("swb", dma=True)
                idx = sb("idx", [128, 12], I32); r_idx = S.res("idx", dma=True)
                idb = sb("idb", [128, 128], BF16); idf = sb("idf", [128, 128], F32); r_id = S.res("idB", dma=True)
                hkL = sb("hkL", [128, 3, 384], BF16); hkR = sb("hkR", [128, 3, 384], BF16)
                hvL = sb("hvL", [128, 3, 390], BF16); hvR = sb("hvR", [128, 3, 390], BF16); r_halo = S.res("haloB", dma=True)
                ktw2 = sb("ktw2", [128, 3, 10 * 128], BF16)
                vw = sb("vw", [128, 10, 390], BF16); r_win = S.res("winB", dma=True)
                qna = [sb("qna%d" % i, [128, 2, 512], BF16) for i in range(2)]
                qsw = [sb("qsw%d" % i, [128, 4, 384], BF16) for i in range(2)]
                qax = [sb("qax%d" % i, [128, 3, 512], BF16) for i in range(2)]
                r_q = [S.res("q%d" % i, dma=True) for i in range(2)]
                xr = [sb("xr%d" % i, [128, D], F32) for i in range(2)]; r_xr = [S.res("xrB%d" % i, dma=True) for i in range(2)]
                pt = [sb("pt%d" % i, [128, 1024], BF16) for i in range(3)]; r_pt = [S.res("pt%d" % i) for i in range(3)]
                otk = sb("otk", [128, 4, 16, 65], F32); r_otk = [S.res("otk%d" % b) for b in range(4)]
                oTs = [sb("oTs%d" % i, [65, 512], F32) for i in range(2)]; r_oTs = [S.res("oTs%d" % i) for i in range(2)]
                den2 = [sb("den%d" % i, [128, 48], F32) for i in range(2)]; r_den2 = [S.res("den%d" % i) for i in range(2)]
                onr2 = [sb("onr%d" % i, [128, D], F32) for i in range(2)]; r_onr2 = [S.res("onr%d" % i) for i in range(2)]
                yb2 = [sb("yb%d" % i, [128, D], F32) for i in range(2)]; r_yb2 = [S.res("yb%d" % i) for i in range(2)]
                yT2 = [sb("yT%d" % i, [128, 8, 128], BF16) for i in range(2)]; r_yT2 = [S.res("yT%d" % i) for i in range(2)]
                tmp2 = [sb("tmpB%d" % i, [128, D], F32) for i in range(2)]; r_tmp2 = [S.res("tmpB%d" % i) for i in range(2)]
                pS = [ps("pS%d" % i, [128, 1024], F32) for i in range(2)]; r_pS = [S.res("pS%d" % i) for i in range(2)]
                pO = [ps("pO%d" % i, [128, 512], F32) for i in range(2)]; r_pO = [S.res("pO%d" % i) for i in range(2)]
                pna = ps("pna", [128, 4, 65], F32); r_pna = S.res("pna")
                psw = ps("psw", [128, 6, 65], F32); r_psw = S.res("psw")

                for kc in range(8):
                    S.dma("pool", wo[:, kc, :], w_o[l, kc * 128:(kc + 1) * 128, :], [R_in], [r_wo])
                S.dma("sp", gta[:], modrow[l:l + 1, 2 * D:3 * D].partition_broadcast(128), [R["modrow"]], [r_bc])
                S.dma("sp", ggr[:], g_group[l:l + 1, :].partition_broadcast(128), [R_in], [r_bc])
                S.dma("sp", esk[:], sink[l:l + 1, :].partition_broadcast(128), [R_in], [r_esk])
                S.op("act", lambda E: E.activation(out=esk[:], in_=esk[:], func=AF.Exp), [r_esk], [r_esk])
                cur_var = [0]
                S.dma("pool", nab[:], na_tab[l, 0], [R_in], [r_nab])
                for t in range(5):
                    S.dma("pool", swb[:, t, :], sw_tab[t], [R_in], [r_swb])
                S.dma("sp", idx[:], idx_in, [R_in], [r_idx])
                S.dma("sp", idb[:], idb_in, [R_in], [r_id])
                S.dma("sp", idf[:], idf_in, [R_in], [r_id])
                for c in range(3):
                    S.dma("pool", hkL[:, c, :], hk_all.ap()[:, :], [R_gath, r_idx], [r_halo],
                          indirect=bass.IndirectOffsetOnAxis(ap=idx[:, c:c + 1], axis=0))
                    S.dma("pool", hkR[:, c, :], hk_all.ap()[:, :], [R_gath, r_idx], [r_halo],
                          indirect=bass.IndirectOffsetOnAxis(ap=idx[:, 3 + c:4 + c], axis=0))
                    S.dma("pool", hvL[:, c, :], hv_all.ap()[:, :], [R_gath, r_idx], [r_halo],
                          indirect=bass.IndirectOffsetOnAxis(ap=idx[:, 6 + c:7 + c], axis=0))
                    S.dma("pool", hvR[:, c, :], hv_all.ap()[:, :], [R_gath, r_idx], [r_halo],
                          indirect=bass.IndirectOffsetOnAxis(ap=idx[:, 9 + c:10 + c], axis=0))

                def load_q(G):
                    s = G % 2
                    gsl = slice(G * 512, (G + 1) * 512)
                    S.dma("sp", qna[s][:], q_na[:, :, gsl].rearrange("c p t -> p c t"), [R["qscr"]], [r_q[s]])
                    S.dma("sp", qsw[s][:], q_sw[:, G * 4:(G + 1) * 4, :, :].rearrange("p b r t -> p b (r t)"), [R["qscr"]], [r_q[s]])
                    S.dma("sp", qax[s][:], q_ax[:, :, gsl].rearrange("c p t -> p c t"), [R["qscr"]], [r_q[s]])

                def load_win(G):
                    lo = max(4 * G - 3, 0)
                    hi = min(4 * G + 7, NB)
                    w0 = lo - (4 * G - 3)
                    n = hi - lo
                    S.dma("sp", ktw2[:, :, w0 * 128:(w0 + n) * 128], ktns[:, :, lo * 128:hi * 128].rearrange("c p t -> p c t"),
                          [R["kvscr"]], [r_win])
                    S.dma("sp", vw[:, w0:w0 + n, :], vns[lo * 128:hi * 128, :].rearrange("(b p) c -> p b c", p=128),
                          [R["kvscr"]], [r_win])

                def kt_blk(c, m, G):
                    if m < 0:
                        return hkL[:, c, (m + 3) * 128:(m + 4) * 128], r_halo
                    if m >= NB:
                        return hkR[:, c, (m - NB) * 128:(m - NB + 1) * 128], r_halo
                    w = m - (4 * G - 3)
                    return ktw2[:, c, w * 128:(w + 1) * 128], r_win

                def v_blk(m, G):
                    if m < 0:
                        return hvL[:, m + 3, :], r_halo
                    if m >= NB:
                        return hvR[:, m - NB, :], r_halo
                    return vw[:, m - (4 * G - 3), :], r_win

                load_q(0)
                load_win(0)
                for rk in range(4):
                    S.dma("sp", ktax[:, rk * T:(rk + 1) * T], ktax_all.ap()[rk * 128:(rk + 1) * 128, :], [R_gath], [r_ktax])
                for i in range(8):
                    S.dma("sp", vax[:, i * 16:(i + 1) * 16, :],
                          vax_all[i % 2].ap()[(i // 2) * 2048:(i // 2 + 1) * 2048, :].rearrange("(b p) c -> p b c", p=128), [R_gath], [r_vax])
                sp_i = [0]
                pt_i = [0]

                class Unit:
                    pass

                def na_unit(G, s, b, h, var, last_h):
                    n = 4 * G + b
                    u = Unit()
                    c, hp = h // 2, h % 2
                    sp = sp_i[0] % 2; sp_i[0] += 1
                    pi = pt_i[0] % 3; pt_i[0] += 1
                    kbs = list(range(7)) if var != 2 else list(range(1, 6))

                    def fS():
                        if h == 0 and cur_var[0] != var:
                            S.dma("pool", nab[:], na_tab[l, var], [R_in], [r_nab])
                            cur_var[0] = var
                        r_bt = r_nab
                        bt = nab[:].rearrange("p (h k q) -> p h k q", h=4, k=7)
                        for kb in kbs:
                            m = n - 3 + kb
                            kap, r_k = kt_blk(c, m, G)
                            S.op("pe", lambda E, kap=kap, kb=kb: E.matmul(
                                pS[sp][:, kb * 128:(kb + 1) * 128], lhsT=kap[hp * 64:(hp + 1) * 64, :],
                                rhs=qna[s][hp * 64:(hp + 1) * 64, c, b * 128:(b + 1) * 128], start=True, stop=False),
                                 [r_k, r_q[s]], [r_pS[sp]])
                            S.op("pe", lambda E, kb=kb, bt=bt: E.matmul(
                                pS[sp][:, kb * 128:(kb + 1) * 128], lhsT=idb[:], rhs=bt[:, h, kb, :], start=False, stop=True),
                                 [r_bt, r_id], [r_pS[sp]])

                    def fE():
                        S.op("act", lambda E: E.activation(out=pt[pi][:, kbs[0] * 128:(kbs[-1] + 1) * 128],
                                                           in_=pS[sp][:, kbs[0] * 128:(kbs[-1] + 1) * 128], func=AF.Exp),
                             [r_pS[sp]], [r_pt[pi]])

                    def fP():
                        for kb in kbs:
                            m = n - 3 + kb
                            vap, r_v = v_blk(m, G)
                            S.op("pe", lambda E, vap=vap, kb=kb: E.matmul(
                                pna[:, h, :], lhsT=pt[pi][:, kb * 128:(kb + 1) * 128], rhs=vap[:, h * 65:(h + 1) * 65],
                                start=(kb == kbs[0]), stop=(kb == kbs[-1])), [r_v, r_pt[pi]], [r_pna])
                        if last_h:
                            S.op("dve", lambda E: E.tensor_copy(out=otk[:, b, 0:4, :], in_=pna[:]), [r_pna], [r_otk[b]])
                    u.S, u.E, u.P = fS, fE, fP
                    return u

                def sw_unit(G, s, b, g, post):
                    n = 4 * G + b
                    u = Unit()
                    sp = sp_i[0] % 2; sp_i[0] += 1
                    pi = pt_i[0] % 3; pt_i[0] += 1
                    pi2 = pt_i[0] % 3; pt_i[0] += 1

                    def tgt_of(t):
                        if t == 2:
                            return pO[g][:, 0:384], r_pO[g]
                        return pS[sp][:, t * 512:t * 512 + 384], r_pS[sp]

                    def fS():
                        for t in range(3):
                            m = n - 1 + t
                            tv = t
                            if n == 0 and t == 0:
                                tv = 3
                            if n == NB - 1 and t == 2:
                                tv = 4
                            kap, r_k = kt_blk(2, m, G)
                            tgt, r_t = tgt_of(t)
                            S.op("pe", lambda E, kap=kap, tgt=tgt: E.matmul(
                                tgt, lhsT=kap[g * 64:(g + 1) * 64, :], rhs=qsw[s][g * 64:(g + 1) * 64, b, :], start=True, stop=False),
                                 [r_k, r_q[s]], [r_t])
                            S.op("pe", lambda E, tv=tv, tgt=tgt: E.matmul(
                                tgt, lhsT=idb[:], rhs=swb[:, tv, g * 384:(g + 1) * 384], start=False, stop=True),
                                 [r_swb, r_id], [r_t])

                    def fE():
                        S.op("act", lambda E: E.activation(
                            out=pt[pi][:, 0:768].rearrange("p (t q) -> p t q", t=2),
                            in_=pS[sp][:].rearrange("p (t q) -> p t q", t=2)[:, :, 0:384], func=AF.Exp),
                             [r_pS[sp]], [r_pt[pi]])
                        S.op("act", lambda E: E.activation(out=pt[pi2][:, 0:384], in_=pO[g][:, 0:384], func=AF.Exp),
                             [r_pO[g]], [r_pt[pi2]])

                    def fP():
                        for r in range(3):
                            for t in range(3):
                                m = n - 1 + t
                                vap, r_v = v_blk(m, G)
                                if t < 2:
                                    pap, r_p = pt[pi][:, t * 384 + r * 128: t * 384 + (r + 1) * 128], r_pt[pi]
                                else:
                                    pap, r_p = pt[pi2][:, r * 128:(r + 1) * 128], r_pt[pi2]
                                S.op("pe", lambda E, vap=vap, r=r, t=t, pap=pap: E.matmul(
                                    psw[:, g * 3 + r, :], lhsT=pap, rhs=vap[:, 260 + g * 65: 260 + (g + 1) * 65],
                                    start=(t == 0), stop=(t == 2)), [r_v, r_p], [r_psw])
                        if g == 1:
                            S.op("dve", lambda E: E.tensor_copy(out=otk[:, b, 4:10, :], in_=psw[:]), [r_psw], [r_otk[b]])
                        if post is not None:
                            post()
                    u.S, u.E, u.P = fS, fE, fP
                    return u

                def ax_unit(G, s, r, kb):
                    u = Unit()
                    sp = sp_i[0] % 2; sp_i[0] += 1
                    pi = pt_i[0] % 3; pt_i[0] += 1

                    def fS():
                        for g in range(2):
                            S.op("pe", lambda E, g=g: E.matmul(
                                pS[sp][:, g * 512:(g + 1) * 512], lhsT=ktax[g * 64:(g + 1) * 64, kb * 128:(kb + 1) * 128],
                                rhs=qax[s][g * 64:(g + 1) * 64, r, :], start=True, stop=True),
                                 [r_ktax, r_q[s]], [r_pS[sp]])

                    def fE():
                        S.op("act", lambda E: E.activation(out=pt[pi][:], in_=pS[sp][:], func=AF.Exp), [r_pS[sp]], [r_pt[pi]])

                    def fP():
                        for g in range(2):
                            S.op("pe", lambda E, g=g: E.matmul(
                                pO[g][0:65, :], lhsT=vax[:, kb, g * 65:(g + 1) * 65], rhs=pt[pi][:, g * 512:(g + 1) * 512],
                                start=(kb == 0), stop=(kb == 127)), [r_vax, r_pt[pi]], [r_pO[g]])
                        if kb == 127:
                            for g in range(2):
                                S.op("dve", lambda E, g=g: E.tensor_copy(out=oTs[g][:], in_=pO[g][0:65, :]), [r_pO[g]], [r_oTs[g]])
                            for g in range(2):
                                for b in range(4):
                                    S.op("pe", lambda E, b=b, g=g: E.transpose(pna[:, b, :], oTs[g][0:65, b * 128:(b + 1) * 128], idf[0:65, 0:65]),
                                         [r_oTs[g], r_id], [r_pna])
                                hh = 10 + g * 3 + r
                                S.op("dve", lambda E, hh=hh: E.tensor_copy(out=otk[:, :, hh, :], in_=pna[:]), [r_pna], r_otk)
                    u.S, u.E, u.P = fS, fE, fP
                    return u

                def nasw_units(Gq, blocks):
                    sq = Gq % 2
                    us = []
                    for b in blocks:
                        n = 4 * Gq + b
                        var = {0: 0, 1: 1, NB - 2: 3, NB - 1: 4}.get(n, 2)
                        for h in range(4):
                            us.append(na_unit(Gq, sq, b, h, var, h == 3))
                        for g in range(2):
                            post = None
                            if b == 3 and g == 1 and Gq + 1 < NG:
                                post = (lambda Gq=Gq: load_win(Gq + 1))
                            us.append(sw_unit(Gq, sq, b, g, post))
                    return us

                def emit(units, extra=None, per=2):
                    extra = list(extra or [])
                    if units:
                        units[0].S()
                        if len(units) > 1:
                            units[1].S()
                        for i, u in enumerate(units):
                            u.E()
                            if i + 2 < len(units):
                                units[i + 2].S()
                            u.P()
                            for _ in range(per):
                                if extra:
                                    extra.pop(0)()
                    while extra:
                        extra.pop(0)()

                emit(nasw_units(0, [0, 1, 2, 3]))
                for G in range(NG):
                    s = G % 2
                    if G + 1 < NG:
                        load_q(G + 1)
                    units = []
                    for r in range(3):
                        for kb in range(128):
                            units.append(ax_unit(G, s, r, kb))
                    emit(units)
                    def comb_ops(b):
                        n = 4 * G + b
                        k = b % 2
                        dn, r_dn, on_, r_on, yb_, r_yb_, yT_, r_yT_, tm, r_tm = (den2[k], r_den2[k], onr2[k], r_onr2[k], yb2[k], r_yb2[k],
                                                                                 yT2[k], r_yT2[k], tmp2[k], r_tmp2[k])
                        xs = k
                        ops = []
                        ops.append(lambda: S.dma("sp", xr[xs][:], x_src[n * 128:(n + 1) * 128, :], [r_xsrc], [r_xr[xs]]))
                        ops.append(lambda: S.op("dve", lambda E: E.tensor_copy(out=dn[:, 0:16], in_=otk[:, b, :, 64]), [r_otk[b]], [r_dn]))
                        ops.append(lambda: S.op("dve", lambda E: E.tensor_tensor(out=dn[:, 4:10], in0=dn[:, 4:10], in1=esk[:], op=ALU.add),
                                                [r_dn, r_esk], [r_dn]))
                        ops.append(lambda: S.op("dve", lambda E: E.reciprocal(out=dn[:, 16:32], in_=dn[:, 0:16]), [r_dn], [r_dn]))
                        ops.append(lambda: S.op("pool", lambda E: E.tensor_tensor(
                            out=on_[:].rearrange("p (h d) -> p h d", d=64), in0=otk[:, b, :, 0:64],
                            in1=dn[:, 16:32].rearrange("p (h o) -> p h o", o=1).broadcast_to([128, 16, 64]), op=ALU.mult),
                                                [r_otk[b], r_dn], [r_on]))
                        for gi, (lo, hi) in enumerate(((0, 256), (256, 640), (640, 1024))):
                            ops.append(lambda gi=gi, lo=lo, hi=hi: S.op("act", lambda E: E.activation(
                                out=tm[:, lo:hi], in_=on_[:, lo:hi], func=AF.Square, accum_out=dn[:, 32 + gi:33 + gi]),
                                                                        [r_on], [r_tm, r_dn]))
                        ops.append(lambda: S.op("dve", lambda E: E.tensor_scalar(out=dn[:, 36:37], in0=dn[:, 32:33], scalar1=1.0 / 256, scalar2=EPS,
                                                                                 op0=ALU.mult, op1=ALU.add), [r_dn], [r_dn]))
                        ops.append(lambda: S.op("dve", lambda E: E.tensor_scalar(out=dn[:, 37:39], in0=dn[:, 33:35], scalar1=1.0 / 384, scalar2=EPS,
                                                                                 op0=ALU.mult, op1=ALU.add), [r_dn], [r_dn]))
                        ops.append(lambda: S.op("act", lambda E: E.activation(out=dn[:, 36:39], in_=dn[:, 36:39], func=AF.Sqrt), [r_dn], [r_dn]))
                        ops.append(lambda: S.op("dve", lambda E: E.reciprocal(out=dn[:, 40:43], in_=dn[:, 36:39]), [r_dn], [r_dn]))
                        for gi, (lo, hi) in enumerate(((0, 256), (256, 640), (640, 1024))):
                            ops.append(lambda gi=gi, lo=lo, hi=hi: S.op("dve", lambda E: E.scalar_tensor_tensor(
                                out=yb_[:, lo:hi], in0=on_[:, lo:hi], scalar=dn[:, 40 + gi:41 + gi], in1=ggr[:, lo:hi],
                                op0=ALU.mult, op1=ALU.mult), [r_on, r_dn, r_bc], [r_yb_]))

                        def f_tr():
                            for kc in range(8):
                                S.op("pe", lambda E, kc=kc: E.transpose(pS[k][:, kc * 128:(kc + 1) * 128],
                                                                        yb_[:, kc * 128:(kc + 1) * 128], idf[:]),
                                     [r_yb_, r_id], [r_pS[k]])
                        ops.append(f_tr)
                        ops.append(lambda: S.op("act", lambda E: E.activation(
                            out=yT_[:, 0:4, :], in_=pS[k][:, 0:512].rearrange("p (k t) -> p k t", k=4), func=AF.Copy), [r_pS[k]], [r_yT_]))
                        ops.append(lambda: S.op("dve", lambda E: E.tensor_copy(
                            out=yT_[:, 4:8, :], in_=pS[k][:, 512:1024].rearrange("p (k t) -> p k t", k=4)), [r_pS[k]], [r_yT_]))

                        def f_mm():
                            for hf in range(2):
                                for kc in range(8):
                                    S.op("pe", lambda E, kc=kc, hf=hf: E.matmul(pS[k][:, hf * 512:(hf + 1) * 512], lhsT=yT_[:, kc, :],
                                                                               rhs=wo[:, kc, hf * 512:(hf + 1) * 512],
                                                                               start=(kc == 0), stop=(kc == 7)),
                                         [r_yT_, r_wo], [r_pS[k]])
                        ops.append(f_mm)
                        ops.append(lambda: S.op("dve", lambda E: E.tensor_tensor(out=tm[:], in0=pS[k][:], in1=gta[:], op=ALU.mult),
                                                [r_pS[k], r_bc], [r_tm]))
                        ops.append(lambda: S.op("pool", lambda E: E.tensor_tensor(out=xr[xs][:], in0=xr[xs][:], in1=tm[:], op=ALU.add),
                                                [r_xr[xs], r_tm], [r_xr[xs]]))
                        ops.append(lambda: S.dma("sp", x1[n * 128:(n + 1) * 128, :], xr[xs][:], [r_xr[xs]], [R["x1"]]))
                        return ops

                    NFRONT = 15

                    def run_pair(o0, o1):
                        for f0, f1 in zip(o0, o1):
                            f0()
                            f1()

                    def zipped(o0, o1):
                        out = []
                        for f0, f1 in zip(o0, o1):
                            out += [f0, f1]
                        return out

                    NPRE = 5
                    c0, c1 = comb_ops(0), comb_ops(1)
                    run_pair(c0[:NPRE], c1[:NPRE])
                    emit(nasw_units(G + 1, [0, 1]) if G + 1 < NG else [], zipped(c0[NPRE:NFRONT], c1[NPRE:NFRONT]), per=2)
                    run_pair(c0[NFRONT:], c1[NFRONT:])
                    c2, c3 = comb_ops(2), comb_ops(3)
                    run_pair(c2[:NPRE], c3[:NPRE])
                    emit(nasw_units(G + 1, [2, 3]) if G + 1 < NG else [], zipped(c2[NPRE:NFRONT], c3[NPRE:NFRONT]), per=2)
                    run_pair(c2[NFRONT:], c3[NFRONT:])
                S.barrier()

        def phase_c(l, x_dst, r_xdst, last):
            with ExitStack() as st:
                sb = lambda n, s, d: st.enter_context(nc.sbuf_tensor(U(n), s, d))
                ps = lambda n, s, d: st.enter_context(nc.psum_tensor(U(n), s, d))
                wgu = sb("wgu", [128, 8, 2 * F], BF16); r_wgu = S.res("wgu", dma=True)
                wdn = sb("wdn", [128, NFC, D], BF16); r_wdn = S.res("wdn", dma=True)
                idb = sb("idb", [128, 128], BF16); r_idb = S.res("idbC", dma=True)
                gtf = sb("gtf", [128, D], F32); gfin = sb("gfin", [128, D], F32); r_bc = S.res("bcC", dma=True)
                a1, a0, r_a = load_mod_cols(st, l, 4 * D, 3 * D, g_ffn_t, "C")
                xt = [sb("xtc%d" % i, [128, D], F32) for i in range(4)]; r_xt = [S.res("xtc%d" % i, dma=True) for i in range(4)]
                ssq = sb("ssqc", [128, 16], F32); r_ssq = S.res("ssqc")
                fsq = sb("fsq", [128, 4], F32); r_fsq = S.res("fsq")
                xo = sb("xoc", [128, D], F32); r_xo = S.res("xoc", dma=True)
                xn = sb("xnc", [128, 4, D], BF16); r_xn = [S.res("xnc%d" % b) for b in range(4)]
                hT = sb("hTc", [128, 8, 512], BF16); r_hT = S.res("hTc")
                actT = sb("actT", [128, NFC, 512], BF16); r_actT = [S.res("actT%d" % j) for j in range(NFC)]
                sg = [sb("sg%d" % i, [128, 512], F32) for i in range(2)]; r_sg = [S.res("sg%d" % i) for i in range(2)]
                tmp = sb("tmpC", [128, D], F32); r_tmp = S.res("tmpC")
                psT = [ps("psTc%d" % i, [128, 1024], BF16) for i in range(2)]; r_psT = [S.res("psTc%d" % i) for i in range(2)]
                pg = [ps("pg%d" % i, [128, 512], F32) for i in range(2)]; r_pg = [S.res("pg%d" % i) for i in range(2)]
                pu = [ps("pu%d" % i, [128, 512], F32) for i in range(2)]; r_pu = [S.res("pu%d" % i) for i in range(2)]
                pd = ps("pd", [128, 1024], F32); r_pd = S.res("pd")

                S.dma("sp", idb[:], idb_in, [R_in], [r_idb])
                S.dma("sp", gtf[:], modrow[l:l + 1, 5 * D:6 * D].partition_broadcast(128), [R["modrow"]], [r_bc])
                S.dma("sp", gfin[:], g_final[0:1, :].partition_broadcast(128), [R_in], [r_bc])
                for half in range(2):
                    for hf in range(2):
                        for kc in range(8):
                            c0 = hf * F + half * 1408
                            S.dma("pool", wgu[:, kc, c0:c0 + 1408], w_gu[l, kc * 128:(kc + 1) * 128, c0:c0 + 1408], [R_in], [r_wgu])
                for j in range(NFC):
                    S.dma("pool", wdn[:, j, :], w_down[l, j * 128:(j + 1) * 128, :], [R_in], [r_wdn])

                def load_x(G):
                    for b in range(4):
                        S.dma("sp", xt[b][:], x1[(G * 4 + b) * 128:(G * 4 + b + 1) * 128, :], [R["x1"]], [r_xt[b]])

                def nops():
                    return norm_ops([xt[b][:] for b in range(4)], r_xt, 4, ssq, r_ssq, None, None, xn, r_xn)

                def do_tr():
                    tr_part(4, xn, r_xn, psT, r_psT, hT, r_hT, a1, a0, r_a, idb, r_idb)

                load_x(0)
                for f in nops():
                    f()
                do_tr()
                for G in range(NG):
                    pend = []
                    if G + 1 < NG:
                        load_x(G + 1)
                        pend = nops()
                    for j in range(NFC):
                        p = j % 2
                        for kc in range(8):
                            S.op("pe", lambda E, j=j, kc=kc, p=p: E.matmul(pg[p][:], lhsT=wgu[:, kc, j * 128:(j + 1) * 128], rhs=hT[:, kc, :],
                                                                          start=(kc == 0), stop=(kc == 7)), [r_wgu, r_hT], [r_pg[p]])
                        for kc in range(8):
                            S.op("pe", lambda E, j=j, kc=kc, p=p: E.matmul(pu[p][:], lhsT=wgu[:, kc, F + j * 128:F + (j + 1) * 128], rhs=hT[:, kc, :],
                                                                          start=(kc == 0), stop=(kc == 7)), [r_wgu, r_hT], [r_pu[p]])
                        S.op("act", lambda E, p=p: E.activation(out=sg[p][:], in_=pg[p][:], func=AF.Silu), [r_pg[p]], [r_sg[p]])
                        S.op("dve", lambda E, j=j, p=p: E.tensor_tensor(out=actT[:, j, :], in0=pu[p][:], in1=sg[p][:], op=ALU.mult),
                             [r_pu[p], r_sg[p]], [r_actT[j]])
                        if j >= 3 and pend:
                            pend.pop(0)()
                    while pend:
                        pend.pop(0)()
                    if G + 1 < NG:
                        do_tr()
                    for b in range(4):
                        n = G * 4 + b
                        S.dma("sp", xo[:], x1[n * 128:(n + 1) * 128, :], [R["x1"]], [r_xo])
                        for hf in range(2):
                            for j in range(NFC):
                                S.op("pe", lambda E, j=j, b=b, hf=hf: E.matmul(pd[:, hf * 512:(hf + 1) * 512], lhsT=actT[:, j, b * 128:(b + 1) * 128],
                                                                              rhs=wdn[:, j, hf * 512:(hf + 1) * 512],
                                                                              start=(j == 0), stop=(j == NFC - 1)),
                                     [r_actT[j], r_wdn], [r_pd])
                        S.op("dve", lambda E: E.tensor_tensor(out=tmp[:], in0=pd[:], in1=gtf[:], op=ALU.mult), [r_pd, r_bc], [r_tmp])
                        S.op("pool", lambda E: E.tensor_tensor(out=xo[:], in0=xo[:], in1=tmp[:], op=ALU.add), [r_xo, r_tmp], [r_xo])
                        if last:
                            S.op("act", lambda E: E.activation(out=tmp[:], in_=xo[:], func=AF.Square, accum_out=fsq[:, 0:1]),
                                 [r_xo], [r_tmp, r_fsq])
                            S.op("dve", lambda E: E.tensor_scalar(out=fsq[:, 1:2], in0=fsq[:, 0:1], scalar1=1.0 / D, scalar2=EPS,
                                                                  op0=ALU.mult, op1=ALU.add), [r_fsq], [r_fsq])
                            S.op("act", lambda E: E.activation(out=fsq[:, 1:2], in_=fsq[:, 1:2], func=AF.Sqrt), [r_fsq], [r_fsq])
                            S.op("dve", lambda E: E.reciprocal(out=fsq[:, 2:3], in_=fsq[:, 1:2]), [r_fsq], [r_fsq])
                            S.op("dve", lambda E: E.scalar_tensor_tensor(out=xo[:], in0=xo[:], scalar=fsq[:, 2:3], in1=gfin[:],
                                                                         op0=ALU.mult, op1=ALU.mult), [r_xo, r_fsq, r_bc], [r_xo])
                        S.dma("sp", x_dst[n * 128:(n + 1) * 128, :], xo[:], [r_xo], [r_xdst])
                S.barrier()

        def norm_transpose_c(xts, r_xts, ssq, r_ssq, junk, r_junk, xn, r_xn, psT, r_psT, hT, r_hT, a1, a0, r_a, idb, r_idb):
            norm_transpose(xts, r_xts, 4, ssq, r_ssq, junk, r_junk, xn, r_xn, psT, r_psT, hT, r_hT, a1, a0, r_a, idb, r_idb)

        phase_mod()
        x_src, r_xsrc = x_in, R_in
        done = False
        for l in range(L):
            if done:
                break
            phase_a(l, x_src, r_xsrc)
            if stop_after == ("a", l):
                done = True
                break
            exchange()
            if stop_after == ("x", l):
                done = True
                break
            phase_b(l, x_src, r_xsrc)
            if stop_after == ("b", l):
                done = True
                break
            last = (l == L - 1)
            if last:
                phase_c(l, out, R["out"], True)
            else:
                phase_c(l, x2, R["x2"], False)
            if stop_after == ("c", l):
                done = True
                break
            x_src, r_xsrc = x2, R["x2"]
        if dbg:
            srcs = {"q_na": q_na, "q_sw": q_sw, "q_ax": q_ax, "ktns": ktns, "vns": vns, "ktax_all": ktax_all.ap(),
                    "vax_all": vax_all[0].ap(), "hk_all": hk_all.ap(), "hv_all": hv_all.ap(), "x1": x1, "x2": x2, "modrow": modrow,
                    "ktax_loc": ktax_loc.ap(), "vax_loc": vax_loc[0].ap()}
            for dn in dbg_outs:
                S.dma("sp", dbg_outs[dn], srcs[dn], [R["qscr"], R["kvscr"], R["x1"], R["x2"], R["modrow"], R_gath], [R["dbg"]])
        fin = {}
        for k in (R["out"], R["dbg"]):
            if k.w is not None:
                fin[k.w[0]] = k.w[1]
        S._wait("sp", fin)
    return nc


def make_in_maps(x, c, w_mod, b_mod, g_attn, w_in, rpb_na, sink_sw, t5_table, gq_ax, gk_ax,
                 g_group, w_o, g_ffn, w_gu, w_down, g_final):
    f32 = lambda a: np.ascontiguousarray(np.asarray(a, dtype=np.float32))
    x = f32(x); c = f32(c)
    perm = _win_perm()
    w_in_p = np.ascontiguousarray(f32(w_in)[:, :, perm])
    bidx = _t5_bucket_idx()
    gqk = np.concatenate([np.tile(f32(gq_ax), (1, 6)), np.tile(f32(gk_ax), (1, 2))], axis=1)
    tcol = lambda g: np.ascontiguousarray(f32(g).reshape(L, 8, 128).transpose(0, 2, 1))
    shared = {
        "w_mod": f32(w_mod), "b_mod": f32(b_mod), "g_attn_t": tcol(g_attn), "g_ffn_t": tcol(g_ffn),
        "w_in": w_in_p, "w_o": f32(w_o), "w_gu": f32(w_gu), "w_down": f32(w_down), "g_group": f32(g_group),
        "gqk": np.ascontiguousarray(gqk), "g_final": f32(g_final).reshape(1, D), "sink": f32(sink_sw),
        "idb_in": np.eye(128, dtype=np.float32).astype(ml_dtypes.bfloat16), "idf_in": np.eye(128, dtype=np.float32),
    }
    rpb = f32(rpb_na); t5 = f32(t5_table)
    per_q = {}
    for q in range(4):
        rc, rs = _rope_tables(q)
        na = np.stack([_na_tables(rpb[l], q) for l in range(L)], 0).reshape(L, 5, 128, 4 * 7 * 128)
        per_q[q] = {"rope_c": rc, "rope_s": rs, "na_tab": np.ascontiguousarray(na),
                    "sw_tab": np.ascontiguousarray(_sw_tables(t5, q, bidx).reshape(5, 128, 768)), "idx_in": _idx_table(q)}
    in_maps = []
    for i in range(8):
        b, q = i // 4, i % 4
        m = dict(shared)
        m.update(per_q[q])
        m["x_in"] = np.ascontiguousarray(x[b, q * T:(q + 1) * T, :])
        m["c_t"] = np.ascontiguousarray(c[b].reshape(8, 128).T)
        in_maps.append(m)
    return in_maps


_NC_CACHE = {}


def kernel(**inputs):
    in_maps = make_in_maps(**inputs)
    if "nc" not in _NC_CACHE:
        _NC_CACHE["nc"] = build()
    nc = _NC_CACHE["nc"]
    res = run_bass_kernel_spmd(nc, in_maps, core_ids=list(range(8)))
    outp = np.empty((2, 4 * T, D), np.float32)
    for i in range(8):
        b, q = i // 4, i % 4
        outp[b, q * T:(q + 1) * T, :] = res.results[i]["out"]
    return outp
```
